# Optimizing a Trainium2 kernel written in Bass

```python
import numpy as np
import jax
import jax.numpy as jnp
from jax import lax

D_MODEL = 1024
BATCH = 16
SEQ = 2048
DEPTH = 2

HEAD_DIM = 64
ROPE_THETA = 10000.0
NORM_EPS = 1e-6
NEG = -1e30
BRANCH_W = D_MODEL // 2
N_BRANCH = 4
NSA_HEADS = BRANCH_W // HEAD_DIM
NSA_KV = 2
NSA_R = NSA_HEADS // NSA_KV
CMP_LEN = 32
CMP_STRIDE = 16
CMP_HIDDEN = 256
SEL_LEN = 64
SEL_TOPK = 16
NSA_WINDOW = 512
SEL_Q_CHUNK = 32
FORCE_BONUS = 1e3
SWA_HEADS = BRANCH_W // HEAD_DIM
SWA_KV = 2
SWA_R = SWA_HEADS // SWA_KV
SWA_WINDOW = 128
Q_BLOCK = 128
CONV_CH = BRANCH_W
CONV_WIDTH = 31
POOL_CH = BRANCH_W
POOL_WINDOWS = (2, 4, 8, 16)
POOL_GROUPS = 4
POOL_GROUP_CH = POOL_CH // POOL_GROUPS
D_FF = ((8 * D_MODEL // 3) + 127) // 128 * 128
FFN_CONV_WIDTH = 3
IN_SIZES = (
    NSA_HEADS * HEAD_DIM,
    3 * 2 * NSA_KV * HEAD_DIM,
    3 * NSA_HEADS,
    SWA_HEADS * HEAD_DIM,
    2 * SWA_KV * HEAD_DIM,
    2 * CONV_CH,
    POOL_CH,
)
IN_COLS = sum(IN_SIZES)

kernel_name = "hybrid_nsa_conformer_pool_swa_block"


def rms_norm(x, g):
    xf = x.astype(jnp.float32)
    y = xf * lax.rsqrt(jnp.mean(xf * xf, axis=-1, keepdims=True) + NORM_EPS)
    return (y * g.astype(jnp.float32)).astype(x.dtype)


def layer_norm(x, g, b):
    xf = x.astype(jnp.float32)
    mu = jnp.mean(xf, axis=-1, keepdims=True)
    var = jnp.mean(jnp.square(xf - mu), axis=-1, keepdims=True)
    y = (xf - mu) * lax.rsqrt(var + NORM_EPS)
    return (y * g.astype(jnp.float32) + b.astype(jnp.float32)).astype(x.dtype)


def rope_tables(seq):
    inv = 1.0 / (ROPE_THETA ** (jnp.arange(0, HEAD_DIM, 2, dtype=jnp.float32) / HEAD_DIM))
    ang = jnp.arange(seq, dtype=jnp.float32)[:, None] * inv[None, :]
    return jnp.cos(ang), jnp.sin(ang)


def apply_rope(x, cos, sin):
    shp = (1, x.shape[1]) + (1,) * (x.ndim - 3) + (HEAD_DIM // 2,)
    c, s = cos.reshape(shp), sin.reshape(shp)
    xf = x.astype(jnp.float32)
    x1, x2 = xf[..., : HEAD_DIM // 2], xf[..., HEAD_DIM // 2:]
    return jnp.concatenate([x1 * c - x2 * s, x2 * c + x1 * s], axis=-1).astype(x.dtype)


def causal_depthwise_conv(x, w, b):
    k, c = w.shape
    y = lax.conv_general_dilated(
        x, w[:, None, :].astype(x.dtype), window_strides=(1,), padding=[(k - 1, 0)],
        dimension_numbers=("NWC", "WIO", "NWC"), feature_group_count=c)
    return y + b.astype(x.dtype)


def masked_softmax(s, mask):
    p = jax.nn.softmax(jnp.where(mask, s, NEG), axis=-1)
    return jnp.where(mask, p, 0.0)


def banded_attention(q, k, v, window, sinks):
    bsz, seq, g, r, dk = q.shape
    span = window + Q_BLOCK
    k_pad = jnp.pad(k, ((0, 0), (window, 0), (0, 0), (0, 0)))
    v_pad = jnp.pad(v, ((0, 0), (window, 0), (0, 0), (0, 0)))
    scale = dk ** -0.5

    def block(i):
        start = i * Q_BLOCK
        qb = lax.dynamic_slice_in_dim(q, start, Q_BLOCK, axis=1)
        kb = lax.dynamic_slice_in_dim(k_pad, start, span, axis=1)
        vb = lax.dynamic_slice_in_dim(v_pad, start, span, axis=1)
        t = start + jnp.arange(Q_BLOCK)
        j = start - window + jnp.arange(span)
        mask = (j[None, :] <= t[:, None]) & (j[None, :] > t[:, None] - window) & (j[None, :] >= 0)
        mask = mask[None, None, None]
        s = jnp.einsum("bqgrd,bkgd->bgrqk", qb, kb, preferred_element_type=jnp.float32) * scale
        if sinks is None:
            p = masked_softmax(s, mask)
        else:
            sink = sinks.astype(jnp.float32).reshape(g, r)[None, :, :, None, None]
            s = jnp.where(mask, s, NEG)
            m = jnp.maximum(jnp.max(s, axis=-1, keepdims=True), sink)
            e = jnp.exp(s - m)
            p = e / (jnp.sum(e, axis=-1, keepdims=True) + jnp.exp(sink - m))
        return jnp.einsum("bgrqk,bkgd->bqgrd", p.astype(v.dtype), vb)

    out = lax.map(block, jnp.arange(seq // Q_BLOCK))
    return jnp.moveaxis(out, 0, 1).reshape(bsz, seq, g, r, dk)


def nsa_compress(k, pos, w1, w2):
    bsz, seq, g, dk = k.shape
    n_cmp = (seq - CMP_LEN) // CMP_STRIDE + 1
    idx = np.arange(n_cmp)[:, None] * CMP_STRIDE + np.arange(CMP_LEN)[None, :]
    blk = jnp.swapaxes(k[:, idx], 2, 3) + pos.astype(k.dtype)
    flat = blk.reshape(bsz, n_cmp, g, CMP_LEN * dk)
    return jax.nn.gelu(flat @ w1, approximate=True) @ w2


def nsa_compressed_attention(q, kc, vc):
    seq, n_cmp = q.shape[1], kc.shape[1]
    t = jnp.arange(seq)
    blk_end = jnp.arange(n_cmp) * CMP_STRIDE + CMP_LEN - 1
    mask = (blk_end[None, :] <= t[:, None])[None, :, None, None, :]
    s = jnp.einsum("bsgrd,bngd->bsgrn", q, kc, preferred_element_type=jnp.float32) * HEAD_DIM ** -0.5
    p = masked_softmax(s, mask)
    o = jnp.einsum("bsgrn,bngd->bsgrd", p.astype(vc.dtype), vc)
    return o, p


def nsa_select_blocks(p_cmp, seq):
    n_cmp = p_cmp.shape[-1]
    n_slc = seq // SEL_LEN
    i = jnp.arange(n_cmp)
    j = jnp.arange(n_slc)
    overlap = ((i[:, None] * CMP_STRIDE < (j[None, :] + 1) * SEL_LEN)
               & (i[:, None] * CMP_STRIDE + CMP_LEN > j[None, :] * SEL_LEN)).astype(jnp.float32)
    imp = jnp.einsum("bsgrn,nj->bsgj", p_cmp, overlap)
    cur = jnp.arange(seq) // SEL_LEN
    valid = (j[None, :] <= cur[:, None])[None, :, None, :]
    forced = ((j[None, :] == 0) | (j[None, :] == cur[:, None]) | (j[None, :] == cur[:, None] - 1))[None, :, None, :]
    score = jnp.where(valid, imp + jnp.where(forced, FORCE_BONUS, 0.0), -1.0)
    _, sel = lax.top_k(score, min(SEL_TOPK, n_slc))
    return sel


def nsa_selected_attention(q, k, v, sel):
    bsz, seq, g, r, dk = q.shape
    n_slc = seq // SEL_LEN
    n_top = sel.shape[-1]
    kb = k.reshape(bsz, n_slc, SEL_LEN, g, dk).transpose(0, 3, 1, 2, 4)
    vb = v.reshape(bsz, n_slc, SEL_LEN, g, dk).transpose(0, 3, 1, 2, 4)
    nc = seq // SEL_Q_CHUNK
    qc = jnp.moveaxis(q.reshape(bsz, nc, SEL_Q_CHUNK, g, r, dk), 1, 0)
    sc = jnp.moveaxis(sel.reshape(bsz, nc, SEL_Q_CHUNK, g, n_top), 1, 0)
    b_idx = jnp.arange(bsz)[:, None, None, None]
    g_idx = jnp.arange(g)[None, None, :, None]
    scale = dk ** -0.5

    def chunk(args):
        q_c, sel_c, c = args
        t = c * SEL_Q_CHUNK + jnp.arange(SEL_Q_CHUNK)
        kg = kb[b_idx, g_idx, sel_c]
        vg = vb[b_idx, g_idx, sel_c]
        s = jnp.einsum("bcgrd,bcgkld->bcgrkl", q_c, kg, preferred_element_type=jnp.float32) * scale
        kpos = sel_c[..., None] * SEL_LEN + jnp.arange(SEL_LEN)
        mask = (kpos <= t[None, :, None, None, None])[:, :, :, None]
        flat = (bsz, SEL_Q_CHUNK, g, r, n_top * SEL_LEN)
        p = masked_softmax(s.reshape(flat), mask.reshape(bsz, SEL_Q_CHUNK, g, 1, n_top * SEL_LEN)).reshape(s.shape)
        return jnp.einsum("bcgrkl,bcgkld->bcgrd", p.astype(vg.dtype), vg)

    out = lax.map(chunk, (qc, sc, jnp.arange(nc)))
    return jnp.moveaxis(out, 0, 1).reshape(bsz, seq, g, r, dk)


def conformer_conv(u, conv_w, conv_b, ln_g, ln_b):
    a, gte = jnp.split(u, 2, axis=-1)
    c = a * jax.nn.sigmoid(gte)
    c = causal_depthwise_conv(c, conv_w, conv_b)
    return jax.nn.silu(layer_norm(c, ln_g, ln_b))


def pool_mixer(u, w, scale):
    bsz, seq, _ = u.shape
    uf = u.reshape(bsz, seq, POOL_GROUPS, POOL_GROUP_CH).astype(jnp.float32)
    cs = jnp.pad(jnp.cumsum(uf, axis=1), ((0, 0), (1, 0), (0, 0), (0, 0)))
    t = jnp.arange(seq)[:, None]
    win = jnp.array(POOL_WINDOWS, dtype=jnp.int32)[None, :]
    lo = jnp.maximum(t + 1 - win, 0)
    g_idx = jnp.arange(POOL_GROUPS)[None, :]
    mean = (cs[:, 1:] - cs[:, lo, g_idx]) / (t + 1 - lo).astype(jnp.float32)[None, :, :, None]
    pooled = (mean - uf).astype(u.dtype)
    mixed = jnp.einsum("bsgc,gcd->bsgd", pooled, w)
    return mixed.reshape(bsz, seq, POOL_CH) * scale


def setup_inputs(seed: int = 0) -> dict:
    key = jax.random.key(seed)
    ks = jax.random.split(key, 24)
    f32 = jnp.float32
    L = DEPTH

    def nrm(k, shape, scale):
        return jax.random.normal(k, shape, f32) * scale

    return {
        "x": nrm(ks[0], (BATCH, SEQ, D_MODEL), 1.0),
        "norm_mix_pre": 1.0 + nrm(ks[1], (L, D_MODEL), 0.05),
        "norm_mix_post": 1.0 + nrm(ks[2], (L, D_MODEL), 0.05),
        "norm_ffn_pre": 1.0 + nrm(ks[3], (L, D_MODEL), 0.05),
        "norm_ffn_post": 1.0 + nrm(ks[4], (L, D_MODEL), 0.05),
        "w_in": nrm(ks[5], (L, D_MODEL, IN_COLS), D_MODEL ** -0.5),
        "nsa_cmp_pos": nrm(ks[6], (L, 2, CMP_LEN, HEAD_DIM), 0.1),
        "nsa_cmp_w1": nrm(ks[7], (L, 2, CMP_LEN * HEAD_DIM, CMP_HIDDEN), (CMP_LEN * HEAD_DIM) ** -0.5),
        "nsa_cmp_w2": nrm(ks[8], (L, 2, CMP_HIDDEN, HEAD_DIM), CMP_HIDDEN ** -0.5),
        "swa_sinks": nrm(ks[9], (L, SWA_HEADS), 0.5),
        "conv_w": nrm(ks[10], (L, CONV_WIDTH, CONV_CH), CONV_WIDTH ** -0.5),
        "conv_b": nrm(ks[11], (L, CONV_CH), 0.01),
        "conv_ln_g": 1.0 + nrm(ks[12], (L, CONV_CH), 0.05),
        "conv_ln_b": nrm(ks[13], (L, CONV_CH), 0.01),
        "pool_w": nrm(ks[14], (L, POOL_GROUPS, POOL_GROUP_CH, POOL_GROUP_CH), POOL_GROUP_CH ** -0.5),
        "pool_scale": 1.0 + nrm(ks[15], (L, POOL_CH), 0.1),
        "w_branch": nrm(ks[16], (L, N_BRANCH, BRANCH_W, D_MODEL), BRANCH_W ** -0.5),
        "w_gate": nrm(ks[17], (L, D_MODEL, N_BRANCH * D_MODEL), D_MODEL ** -0.5),
        "w_o": nrm(ks[18], (L, D_MODEL, D_MODEL), D_MODEL ** -0.5),
        "ffn_w_up": nrm(ks[19], (L, D_MODEL, 2 * D_FF), D_MODEL ** -0.5),
        "ffn_conv_w": nrm(ks[20], (L, FFN_CONV_WIDTH, 2 * D_FF), FFN_CONV_WIDTH ** -0.5),
        "ffn_conv_b": nrm(ks[21], (L, 2 * D_FF), 0.01),
        "ffn_w_down": nrm(ks[22], (L, D_FF, D_MODEL), D_FF ** -0.5),
    }


def reference(x, norm_mix_pre, norm_mix_post, norm_ffn_pre, norm_ffn_post, w_in,
              nsa_cmp_pos, nsa_cmp_w1, nsa_cmp_w2, swa_sinks, conv_w, conv_b, conv_ln_g,
              conv_ln_b, pool_w, pool_scale, w_branch, w_gate, w_o, ffn_w_up, ffn_conv_w,
              ffn_conv_b, ffn_w_down):
    bsz, seq, _ = x.shape
    cos, sin = rope_tables(seq)
    splits = np.cumsum(IN_SIZES)[:-1].tolist()
    for l in range(DEPTH):
        h = rms_norm(x, norm_mix_pre[l])
        proj = h @ w_in[l]
        a_q, a_kv, a_gate, d_q, d_kv, b_in, c_in = jnp.split(proj, splits, axis=-1)

        qa = a_q.reshape(bsz, seq, NSA_KV, NSA_R, HEAD_DIM)
        kv = a_kv.reshape(bsz, seq, 3, 2, NSA_KV, HEAD_DIM)
        k_cmp = nsa_compress(kv[:, :, 0, 0], nsa_cmp_pos[l, 0], nsa_cmp_w1[l, 0], nsa_cmp_w2[l, 0])
        v_cmp = nsa_compress(kv[:, :, 0, 1], nsa_cmp_pos[l, 1], nsa_cmp_w1[l, 1], nsa_cmp_w2[l, 1])
        o_cmp, p_cmp = nsa_compressed_attention(qa, k_cmp, v_cmp)
        sel = nsa_select_blocks(p_cmp, seq)
        qa_rot = apply_rope(qa, cos, sin)
        o_slc = nsa_selected_attention(qa_rot, apply_rope(kv[:, :, 1, 0], cos, sin), kv[:, :, 1, 1], sel)
        o_win = banded_attention(qa_rot, apply_rope(kv[:, :, 2, 0], cos, sin), kv[:, :, 2, 1], NSA_WINDOW, None)
        ga = jax.nn.sigmoid(a_gate).reshape(bsz, seq, 3, NSA_KV, NSA_R, 1)
        o_a = (ga[:, :, 0] * o_cmp + ga[:, :, 1] * o_slc + ga[:, :, 2] * o_win).reshape(bsz, seq, BRANCH_W)

        o_b = conformer_conv(b_in, conv_w[l], conv_b[l], conv_ln_g[l], conv_ln_b[l])

        o_c = pool_mixer(c_in, pool_w[l], pool_scale[l])

        qd = apply_rope(d_q.reshape(bsz, seq, SWA_KV, SWA_R, HEAD_DIM), cos, sin)
        kvd = d_kv.reshape(bsz, seq, 2, SWA_KV, HEAD_DIM)
        o_d = banded_attention(qd, apply_rope(kvd[:, :, 0], cos, sin), kvd[:, :, 1], SWA_WINDOW,
                               swa_sinks[l]).reshape(bsz, seq, BRANCH_W)

        branches = jnp.stack([o_a, o_b, o_c, o_d], axis=2)
        up = jnp.einsum("bsnc,ncd->bsnd", branches, w_branch[l])
        gates = jax.nn.sigmoid((h @ w_gate[l]).reshape(bsz, seq, N_BRANCH, D_MODEL))
        mix = jnp.einsum("bsnd,bsnd->bsd", gates, up) @ w_o[l]
        x = x + rms_norm(mix, norm_mix_post[l])

        hf = rms_norm(x, norm_ffn_pre[l])
        u = causal_depthwise_conv(hf @ ffn_w_up[l], ffn_conv_w[l], ffn_conv_b[l])
        gate, val = jnp.split(u, 2, axis=-1)
        f = (jax.nn.gelu(gate, approximate=True) * val) @ ffn_w_down[l]
        x = x + rms_norm(f, norm_ffn_post[l])
    return x
```

```python
import numpy as np
from contextlib import ExitStack
import ml_dtypes
import concourse.bass as bass
import concourse.mybir as mybir
from concourse.bass_utils import run_bass_kernel_spmd

F32 = mybir.dt.float32
BF16 = mybir.dt.bfloat16
AF = mybir.ActivationFunctionType
ALU = mybir.AluOpType

SEQ = 2048
DM = 1024
NQT = 4
TQ = 512
NTT = 16
EPS = 1e-6
NEGM = -30000.0
DFF = 2816
NFC = 22
OFF_AQ, OFF_AKV, OFF_AG, OFF_DQ, OFF_DKV, OFF_B, OFF_C = 0, 512, 1280, 1304, 1816, 2072, 3096
NFM = 21
NTM = 408
FFN_TILES = [(0, 510), (510, 510), (1020, 510), (1530, 510), (2040, 8)]


class T:
    __slots__ = ("name", "w", "r", "excl")

    def __init__(self, name="", excl=False):
        self.name = name
        self.w = None
        self.r = {}
        self.excl = excl


class B:
    __slots__ = ("t", "T")

    def __init__(self, t, name="", excl=False):
        self.t = t
        self.T = T(name, excl)


class Sched:
    NDMA = 24

    def __init__(self, nc, es):
        self.nc = nc
        self.es = es
        self.engs = {"pe": nc.tensor, "dve": nc.vector, "act": nc.scalar,
                     "pool": nc.gpsimd, "sp": nc.sync}
        self.semobj = {}
        self.gen = 0
        self.key = {}
        self.cnt = {}
        for k in self.engs:
            self.key[k] = "%s@0" % k
            self.semobj[self.key[k]] = es.enter_context(nc.semaphore("s_%s_0" % k))
            self.cnt[k] = 0
        self.dsem = [es.enter_context(nc.semaphore("d%d" % i)) for i in range(self.NDMA)]
        self.dval = [0] * self.NDMA
        self.dnext = 0
        self.known = {k: {} for k in self.engs}
        for i, sm in enumerate(self.dsem):
            self.semobj["d%d" % i] = sm
        self.ninst = 0

    def rotate(self):
        self.barrier()
        self.gen += 1
        for k in self.engs:
            self.key[k] = "%s@%d" % (k, self.gen)
            self.semobj[self.key[k]] = self.es.enter_context(self.nc.semaphore("s_%s_%d" % (k, self.gen)))
            self.cnt[k] = 0

    def _wait(self, e, tok):
        if tok is None:
            return
        key, val = tok
        if e == "pe" and key.startswith("pe@"):
            return
        if self.known[e].get(key, 0) >= val:
            return
        self.engs[e].wait_ge(self.semobj[key], val)
        self.known[e][key] = val
        self.ninst += 1

    def _deps(self, e, reads, writes):
        for t in reads:
            self._wait(e, t.w)
        for t in writes:
            self._wait(e, t.w)
            for k, v in t.r.items():
                self._wait(e, (k, v))

    def _commit(self, tok, reads, writes):
        for t in reads:
            if t.r.get(tok[0], 0) < tok[1]:
                t.r[tok[0]] = tok[1]
        for t in writes:
            t.w = tok
            t.r = {}

    def op(self, e, fn, reads=(), writes=()):
        if any(t.excl for t in reads):
            writes = list(writes) + [t for t in reads if t.excl]
            reads = [t for t in reads if not t.excl]
        self._deps(e, reads, writes)
        ins = fn()
        self.cnt[e] += 1
        ins.then_inc(self.semobj[self.key[e]], 1)
        tok = (self.key[e], self.cnt[e])
        self.known[e][self.key[e]] = 0 if self.known[e].get(self.key[e]) is None else self.known[e][self.key[e]]
        self._commit(tok, reads, writes)
        self.ninst += 1
        return tok

    def dma(self, out, in_, reads=(), writes=(), q="sp"):
        i = self.dnext
        self.dnext = (self.dnext + 1) % self.NDMA
        key = "d%d" % i
        if self.dval[i] > 0:
            self._wait(q, (key, self.dval[i]))
        self._deps(q, reads, writes)
        ins = self.engs[q].dma_start(out=out, in_=in_)
        self.dval[i] += 16
        ins.then_inc(self.dsem[i], 16)
        tok = (key, self.dval[i])
        self._commit(tok, reads, writes)
        self.ninst += 1
        return tok

    def barrier(self):
        toks = [(self.key[k], self.cnt[k]) for k in self.engs if self.cnt[k] > 0]
        toks += [("d%d" % i, self.dval[i]) for i in range(self.NDMA) if self.dval[i] > 0]
        for e in self.engs:
            for tok in toks:
                if tok[0] != self.key[e]:
                    self._wait(e, tok)

    def finish(self, toks):
        for tok in toks:
            self._wait("sp", tok)


def _swap64(c):
    return np.concatenate([c[32:], c[:32]])


def _fm_blocks():
    ar = np.arange(128)
    a64 = np.arange(64)
    blocks = []

    def rope_pair(A):
        return (A, np.concatenate([_swap64(A[:64]), _swap64(A[64:])]))

    def dup_pair(k):
        return (np.concatenate([k, k]), np.concatenate([_swap64(k), _swap64(k)]))

    for hp in range(4):
        blocks.append(rope_pair(OFF_DQ + hp * 128 + ar))
    for g in range(2):
        blocks.append(dup_pair(OFF_DKV + g * 64 + a64))
    for cc in range(4):
        blocks.append((OFF_B + cc * 128 + ar, OFF_B + 512 + cc * 128 + ar))
    for j in range(2):
        blocks.append((OFF_C + (2 * j) * 128 + ar, OFF_C + (2 * j + 1) * 128 + ar))
    for hp in range(4):
        blocks.append(rope_pair(OFF_AQ + hp * 128 + ar))
    for g in range(2):
        blocks.append(dup_pair(OFF_AKV + ((1 * 2 + 0) * 2 + g) * 64 + a64))
    for g in range(2):
        blocks.append(dup_pair(OFF_AKV + ((2 * 2 + 0) * 2 + g) * 64 + a64))
    blocks.append((OFF_AKV + ar, OFF_AKV + 128 + ar))
    assert len(blocks) == NFM
    return blocks


def _tm_cols():
    ar = np.arange(128)
    return np.concatenate([OFF_AKV + 384 + ar, OFF_AKV + 640 + ar, OFF_DKV + 128 + ar,
                           OFF_AG + np.arange(24)])


def _kmajor(w, ncols):
    kc = w.shape[0] // 128
    return np.ascontiguousarray(w.reshape(kc, 128, ncols).transpose(1, 0, 2).reshape(128, kc * ncols))


def _bf16(a):
    return np.asarray(a, dtype=np.float32).astype(ml_dtypes.bfloat16)


def prep_weights(inp):
    L = inp["w_in"].shape[0]
    f32 = np.float32
    out = {}
    blocks = _fm_blocks()
    tmc = _tm_cols()
    win_fm = np.empty((L, NFM, 128, 2048), f32)
    win_tm = np.empty((L, 128, 8 * NTM), f32)
    for l in range(L):
        w = np.asarray(inp["w_in"][l], f32)
        for b, (A, Bc) in enumerate(blocks):
            win_fm[l, b] = _kmajor(w[:, np.concatenate([A, Bc])], 256)
        win_tm[l] = _kmajor(w[:, tmc], NTM)
    out["win_fm"] = win_fm
    out["win_tm"] = win_tm
    for nm in ("norm_mix_pre", "norm_mix_post", "norm_ffn_pre", "norm_ffn_post"):
        out[nm] = np.ascontiguousarray(np.asarray(inp[nm], f32))
    pos = np.asarray(inp["nsa_cmp_pos"], f32)
    posT = pos.transpose(0, 1, 3, 2)
    out["cmp_pos"] = np.ascontiguousarray(np.concatenate([posT, posT], axis=2))
    w1 = np.asarray(inp["nsa_cmp_w1"], f32).reshape(L, 2, 32, 64, 256).transpose(0, 1, 3, 2, 4)
    w1 = np.concatenate([w1, w1], axis=2)
    out["cmp_w1"] = np.ascontiguousarray(w1.reshape(L, 2, 128, 4, 8 * 256).transpose(0, 1, 3, 2, 4))
    w2 = np.asarray(inp["nsa_cmp_w2"], f32)
    w2k = np.concatenate([w2[:, 0], w2[:, 0]], axis=2)
    out["cmp_w2k"] = np.stack([_kmajor(w2k[l], 128) for l in range(L)])
    out["cmp_w2v"] = np.stack([_kmajor(w2[l, 1], 64) for l in range(L)])
    out["swa_sinks"] = np.ascontiguousarray(np.asarray(inp["swa_sinks"], f32))
    cw = np.asarray(inp["conv_w"], f32)
    out["conv_w"] = np.ascontiguousarray(cw.reshape(L, 31, 4, 128).transpose(0, 3, 2, 1).reshape(L, 128, 124))
    for nm in ("conv_b", "conv_ln_g", "conv_ln_b", "pool_scale"):
        out[nm] = np.ascontiguousarray(np.asarray(inp[nm], f32).reshape(L, 4, 128).transpose(0, 2, 1))
    out["pool_w"] = np.ascontiguousarray(np.asarray(inp["pool_w"], f32).transpose(0, 2, 1, 3).reshape(L, 128, 512))
    wb = np.asarray(inp["w_branch"], f32)
    wg = np.asarray(inp["w_gate"], f32)
    wm = np.empty((L, 8, 4, 128, 1536), f32)
    for l in range(L):
        for dc in range(8):
            for n in range(4):
                wm[l, dc, n, :, 0:512] = _kmajor(wb[l, n][:, dc * 128:(dc + 1) * 128], 128)
                wm[l, dc, n, :, 512:1536] = _kmajor(wg[l][:, n * 1024 + dc * 128: n * 1024 + (dc + 1) * 128], 128)
    out["w_merge"] = wm
    out["w_o"] = np.stack([_kmajor(np.asarray(inp["w_o"][l], f32), 1024) for l in range(L)]).reshape(L, 128, 4, 2048).transpose(0, 2, 1, 3).copy()
    wu = np.asarray(inp["ffn_w_up"], f32)
    wup = np.empty((L, NFC, 128, 2048), f32)
    for l in range(L):
        for c in range(NFC):
            cols = np.concatenate([c * 128 + np.arange(128), DFF + c * 128 + np.arange(128)])
            wup[l, c] = _kmajor(wu[l][:, cols], 256)
    out["w_up"] = wup
    fcw = np.asarray(inp["ffn_conv_w"], f32)
    out["ffn_cw"] = np.ascontiguousarray(fcw.reshape(L, 3, 44, 128).transpose(0, 3, 2, 1).reshape(L, 128, 132))
    out["ffn_cb"] = np.ascontiguousarray(np.asarray(inp["ffn_conv_b"], f32).reshape(L, 44, 128).transpose(0, 2, 1))
    out["w_down"] = np.stack([_kmajor(np.asarray(inp["ffn_w_down"][l], f32), 1024) for l in range(L)]).reshape(L, 128, 11, 2048).transpose(0, 2, 1, 3).copy()
    inv = (1.0 / (np.float32(10000.0) ** (np.arange(0, 64, 2, dtype=f32) / np.float32(64)))).astype(f32)
    ang = (np.arange(SEQ, dtype=f32)[:, None] * inv[None, :]).astype(f32)
    cos, sin = np.cos(ang).astype(f32).T, np.sin(ang).astype(f32).T
    out["rope_c"] = np.ascontiguousarray(np.concatenate([cos, cos, cos, cos], 0))
    out["rope_s"] = np.ascontiguousarray(np.concatenate([-sin, sin, -sin, sin], 0))
    k = np.arange(128)[:, None]
    q = np.arange(512)[None, :]
    msk = np.zeros((17, 128, 512), f32)
    for c in range(4):
        msk[c] = np.where(128 * c + k <= q, 0.0, NEGM)
        msk[4 + c] = np.where(128 * c + k > q, 0.0, NEGM)
    msk[8] = np.where(k > q, 0.0, NEGM)
    for c in range(4):
        msk[9 + c] = np.where((128 * c + k <= q) & (128 * c + k > q - 128), 0.0, NEGM)
    for i in range(4):
        msk[13 + i] = np.where(16 * k + 31 <= 512 * i + q, 0.0, NEGM)
    out["masks"] = _bf16(msk.transpose(1, 0, 2).reshape(128, 17 * 512))
    out["ident"] = _bf16(np.eye(128))
    E = np.zeros((128, 16, 128), f32)
    for kb in range(16):
        for kk in range(128):
            E[2 * kb + kk // 64, kb, kk] = 1.0
    out["sel_e"] = _bf16(E.reshape(128, 2048))
    n = np.arange(128)[:, None]
    j = np.arange(32)[None, :]
    ov = ((n * 16 < (j + 1) * 64) & (n * 16 + 32 > j * 64) & (n < 127)).astype(f32)
    out["overlap"] = _bf16(ov)
    FB = np.zeros((128, 8, 32), f32)
    VM = np.zeros((128, 8, 32), f32)
    for ii in range(2):
        for qs in range(4):
            t = 512 * (ii + 2) + 128 * qs + np.arange(128)
            cur = (t // 64)[:, None]
            jj = np.arange(32)[None, :]
            VM[:, ii * 4 + qs] = (jj <= cur)
            FB[:, ii * 4 + qs] = 1000.0 * ((jj == 0) | (jj == cur) | (jj == cur - 1))
    out["sel_tab"] = np.ascontiguousarray(np.concatenate([FB.reshape(128, 256), VM.reshape(128, 256), (VM - 1.0).reshape(128, 256)], axis=1))
    out["inv16"] = np.ascontiguousarray(np.broadcast_to((1.0 / np.arange(1, 17, dtype=f32))[None, :], (128, 16)))
    return out


class _Stop(Exception):
    pass


def build_program(nseq, nl, wshapes, taps=(), stop_after=None):
    nc = bass.Bass("TRN2", target_bir_lowering=False)
    D = {}
    for nm, (shp, dt) in wshapes.items():
        D[nm] = nc.dram_tensor(nm, list(shp), dt, kind="ExternalInput").ap()
    x_in = nc.dram_tensor("x", [nseq, SEQ, DM], F32, kind="ExternalInput").ap()
    y = nc.dram_tensor("y", [nseq, SEQ, DM], F32, kind="ExternalOutput").ap()
    scr_o = nc.dram_tensor("scr_o", [4, 512, SEQ], BF16).ap()
    tapd = {}
    TAPSHAPES = {"hT": [128, 8, SEQ], "od": [512, SEQ], "ob": [512, SEQ], "oc": [512, SEQ], "oa": [512, SEQ],
                 "x1": [SEQ, DM]}
    for nm in taps:
        tapd[nm] = nc.dram_tensor("tap_" + nm, TAPSHAPES[nm], BF16 if nm in ("hT", "od", "ob", "oc", "oa") else F32,
                                  kind="ExternalOutput").ap()

    with ExitStack() as es:
        S = Sched(nc, es)
        V, A, P, PE = nc.vector, nc.scalar, nc.gpsimd, nc.tensor

        uid = [0]

        def sb(st, name, shape, dt=F32):
            uid[0] += 1
            nm = "%s_u%d" % (name, uid[0])
            return B(st.enter_context(nc.sbuf_tensor(nm, shape, dt)), nm)

        stg = [sb(es, "stg%d" % i, [128, 2048], F32) for i in range(3)]
        wbf = [sb(es, "wbf%d" % i, [128, 2048], BF16) for i in range(3)]
        ident = sb(es, "ident", [128, 128], BF16)
        ones32 = sb(es, "ones32", [128, 128], F32)
        PS = [B(es.enter_context(nc.psum_tensor("ps%d" % i, [128, 512], F32)), "ps%d" % i, True) for i in range(6)]
        PB = [B(es.enter_context(nc.psum_tensor("pb%d" % i, [128, 1024], BF16)), "pb%d" % i, True) for i in range(2)]
        Ty = [[T("y%d_%d" % (s, tt)) for tt in range(NTT)] for s in range(nseq)]
        Tscr = [[T("scr%d_%d" % (n, i)) for i in range(NQT)] for n in range(4)]
        st_i = [0]
        ps_i = [0]
        pb_i = [0]
        fin = []

        S.dma(ident.t[:], D["ident"][:, :], writes=[ident.T])
        S.op("pool", lambda: P.memset(ones32.t[:], 1.0), [], [ones32.T])

        def tapdma(o, i_, reads=()):
            fin.append(S.dma(o, i_, reads=reads))

        stopped = [False]

        def phase_end(name):
            if stop_after == name:
                stopped[0] = True
            return stopped[0]

        def next_ps(lo=0, hi=4):
            i = lo + ps_i[0] % (hi - lo)
            ps_i[0] += 1
            return PS[i]

        def next_pb():
            pb_i[0] += 1
            return PB[pb_i[0] % 2]

        def load_cast(dst_ap, dstT, src_ap, ncols, eng="pool"):
            i = st_i[0] % 3
            st_i[0] += 1
            sg = stg[i]
            S.dma(sg.t[:, 0:ncols], src_ap, writes=[sg.T])
            if eng == "pool":
                S.op("pool", lambda: P.tensor_copy(dst_ap, sg.t[:, 0:ncols]), [sg.T], [dstT])
            elif eng == "dve":
                S.op("dve", lambda: V.tensor_copy(dst_ap, sg.t[:, 0:ncols]), [sg.T], [dstT])
            else:
                S.op("act", lambda: A.activation(out=dst_ap, in_=sg.t[:, 0:ncols], func=AF.Copy), [sg.T], [dstT])

        wb_i = [0]

        def load_wblock(src_ap, ncols, eng="pool"):
            wb = wbf[wb_i[0] % 3]
            wb_i[0] += 1
            load_cast(wb.t[:, 0:ncols], wb.T, src_ap, ncols, eng)
            return wb

        def mm(ps, out_ap, lhsT, rhs, start, stop, reads, skip=False):
            S.op("pe", lambda: PE.matmul(out_ap, lhsT, rhs, start=start, stop=stop, skip_group_check=skip),
                 reads, [ps.T])

        def rms_rstd(st, ss_ap, ssT, n_inv, name):
            l1 = sb(st, name + "_l1", [128, 1])
            r = sb(st, name + "_r", [128, 1])
            return l1, r

        def norm_transpose(st, s, l, src_fn, srcT_fn, gname, hT, hTT, col0):
            gb = sb(st, "gb", [128, DM])
            S.dma(gb.t[:], D[gname][l:l + 1, :].partition_broadcast(128), writes=[gb.T])
            xin = [sb(st, "xin%d" % i, [128, DM]) for i in range(2)]
            junk = sb(st, "junk", [128, DM], BF16)
            hn = [sb(st, "hn%d" % i, [128, DM], BF16) for i in range(2)]
            ss = sb(st, "ss", [128, 1])
            l1 = sb(st, "l1", [128, 1])
            rs = sb(st, "rs", [128, 1])
            for tt in range(NTT):
                xt = xin[tt % 2]
                S.dma(xt.t[:], src_fn(tt), reads=srcT_fn(tt), writes=[xt.T])
                S.op("act", lambda: A.activation(out=junk.t[:], in_=xt.t[:], func=AF.Square, accum_out=ss.t[:, 0:1]),
                     [xt.T], [junk.T, ss.T])
                S.op("act", lambda: A.activation(out=l1.t[:], in_=ss.t[:], func=AF.Ln, scale=1.0 / DM, bias=EPS),
                     [ss.T], [l1.T])
                S.op("act", lambda: A.activation(out=rs.t[:], in_=l1.t[:], func=AF.Exp, scale=-0.5), [l1.T], [rs.T])
                h = hn[tt % 2]
                S.op("dve", lambda: V.scalar_tensor_tensor(h.t[:], xt.t[:], rs.t[:, 0:1], gb.t[:], ALU.mult, ALU.mult),
                     [xt.T, rs.T, gb.T], [h.T])
                pb = next_pb()
                for kc in range(8):
                    S.op("pe", lambda kc=kc: PE.transpose(pb.t[:, kc * 128:(kc + 1) * 128], h.t[:, kc * 128:(kc + 1) * 128], ident.t[:]),
                         [h.T, ident.T], [pb.T])
                c0 = col0 + tt * 128
                S.op("act", lambda: A.activation(out=hT.t[:, :, c0:c0 + 128],
                                                 in_=pb.t[:, :].rearrange("p (k t) -> p k t", k=8), func=AF.Copy),
                     [pb.T], [hTT[tt // 4]])

        def out_epilogue(st, s, l, gname, get_ps, nm):
            gb = sb(st, nm + "gb", [128, DM])
            S.dma(gb.t[:], D[gname][l:l + 1, :].partition_broadcast(128), writes=[gb.T])
            xin = [sb(st, nm + "xin%d" % i, [128, DM]) for i in range(2)]
            tmp = [sb(st, nm + "tmp%d" % i, [128, DM]) for i in range(2)]
            junk = sb(st, nm + "junk", [128, 512], BF16)
            ss = sb(st, nm + "ss", [128, 2])
            s1 = sb(st, nm + "s1", [128, 1])
            l1 = sb(st, nm + "l1", [128, 1])
            rs = sb(st, nm + "rs", [128, 1])
            for tt in range(NTT):
                xt = xin[tt % 2]
                tm = tmp[tt % 2]
                src = x_in if (l == 0 and nm == "mo") else y
                S.dma(xt.t[:], src[s, tt * 128:(tt + 1) * 128, :], reads=[Ty[s][tt]], writes=[xt.T])
                pa, pbk = get_ps(tt)
                for hf, pp in enumerate((pa, pbk)):
                    S.op("act", lambda hf=hf, pp=pp: A.activation(out=junk.t[:], in_=pp.t[:], func=AF.Square,
                                                                  accum_out=ss.t[:, hf:hf + 1]),
                         [pp.T], [junk.T, ss.T])
                S.op("dve", lambda: V.tensor_tensor(s1.t[:], ss.t[:, 0:1], ss.t[:, 1:2], ALU.add), [ss.T], [s1.T])
                S.op("act", lambda: A.activation(out=l1.t[:], in_=s1.t[:], func=AF.Ln, scale=1.0 / DM, bias=EPS),
                     [s1.T], [l1.T])
                S.op("act", lambda: A.activation(out=rs.t[:], in_=l1.t[:], func=AF.Exp, scale=-0.5), [l1.T], [rs.T])
                for hf, pp in enumerate((pa, pbk)):
                    S.op("dve", lambda hf=hf, pp=pp: V.scalar_tensor_tensor(
                        tm.t[:, hf * 512:(hf + 1) * 512], pp.t[:], rs.t[:, 0:1], gb.t[:, hf * 512:(hf + 1) * 512],
                        ALU.mult, ALU.mult), [pp.T, rs.T, gb.T], [tm.T])
                S.op("pool", lambda: P.tensor_tensor(tm.t[:], tm.t[:], xt.t[:], ALU.add), [tm.T, xt.T], [tm.T])
                tok = S.dma(y[s, tt * 128:(tt + 1) * 128, :], tm.t[:], reads=[tm.T], writes=[Ty[s][tt]])
                if nm == "fo" and l == nl - 1:
                    fin.append(tok)

        def mixer(s, l):
            with ExitStack() as mx:
                hT = sb(mx, "hT", [128, 8, SEQ], BF16)
                hTT = [T("hT%d" % i) for i in range(NQT)]
                av = mx.enter_context(ExitStack())
                vall = sb(av, "vall", [128, NTT, 6, 65], BF16)
                ga = sb(av, "ga", [128, NTT, 24])
                with ExitStack() as st:
                    src = x_in if l == 0 else y
                    norm_transpose(st, s, l, lambda tt: src[s, tt * 128:(tt + 1) * 128, :],
                                   lambda tt: [Ty[s][tt]], "norm_mix_pre", hT, hTT, 0)
                    S.barrier()
                if "hT" in tapd and s == 0 and l == 0:
                    tapdma(tapd["hT"][:, :, :], hT.t[:], reads=hTT)
                if phase_end("P0"):
                    return
                with ExitStack() as st:
                    wtm = sb(st, "wtm", [128, 8, NTM], BF16)
                    for hfp in range(2):
                        load_cast(wtm.t[:, hfp * 4:(hfp + 1) * 4, :].rearrange("p a b -> p (a b)"), wtm.T,
                                  D["win_tm"][l, :, hfp * 4 * NTM:(hfp + 1) * 4 * NTM], 4 * NTM)
                    S.op("pool", lambda: P.memset(vall.t[:, :, :, 64:65], 1.0), [], [vall.T])
                    for tt in range(NTT):
                        ps = next_ps()
                        for kc in range(8):
                            mm(ps, ps.t[:, 0:NTM], hT.t[:, kc, tt * 128:(tt + 1) * 128], wtm.t[:, kc, :], kc == 0, kc == 7,
                               [hTT[tt // 4], wtm.T])
                        S.op("act", lambda: A.activation(out=vall.t[:, tt, :, 0:64],
                                                         in_=ps.t[:, 0:384].rearrange("p (a b) -> p a b", a=6), func=AF.Copy),
                             [ps.T], [vall.T])
                        S.op("act", lambda: A.activation(out=ga.t[:, tt, :], in_=ps.t[:, 384:408], func=AF.Sigmoid),
                             [ps.T], [ga.T])
                    S.barrier()

                def proj_pair(wblk, i, reads_extra=()):
                    pa = next_ps()
                    pbk = next_ps()
                    for half, pp in enumerate((pa, pbk)):
                        for kc in range(8):
                            mm(pp, pp.t[:, :], wblk.t[:, kc * 256 + half * 128: kc * 256 + half * 128 + 128],
                               hT.t[:, kc, i * TQ:(i + 1) * TQ], kc == 0, kc == 7, [hTT[i], wblk.T])
                    return pa, pbk

                def rope_evac(st_bufs, pa, pbk, i, dst_ap, dstT, rc, rsn):
                    t1, t2 = st_bufs
                    S.op("dve", lambda: V.tensor_tensor(t1.t[:], pa.t[:], rc.t[:, i * TQ:(i + 1) * TQ], ALU.mult),
                         [pa.T, rc.T], [t1.T])
                    S.op("dve", lambda: V.tensor_tensor(t2.t[:], pbk.t[:], rsn.t[:, i * TQ:(i + 1) * TQ], ALU.mult),
                         [pbk.T, rsn.T], [t2.T])
                    S.op("pool", lambda: P.tensor_tensor(dst_ap, t1.t[:], t2.t[:], ALU.add), [t1.T, t2.T], [dstT])

                def transpose_store(st_bufs, tok_tile, tokT, n, i, tapname):
                    oT = st_bufs
                    for qs in range(4):
                        pb = next_pb()
                        for cc in range(4):
                            S.op("pe", lambda cc=cc: PE.transpose(pb.t[:, cc * 128:(cc + 1) * 128],
                                                                  tok_tile.t[:, qs, cc * 128:(cc + 1) * 128], ident.t[:]),
                                 [tokT, ident.T], [pb.T])
                        S.op("dve", lambda: V.tensor_copy(oT.t[:, :, qs * 128:(qs + 1) * 128],
                                                          pb.t[:, 0:512].rearrange("p (c t) -> p c t", c=4)),
                             [pb.T], [oT.T])
                    for cc in range(4):
                        S.dma(scr_o[n, cc * 128:(cc + 1) * 128, i * TQ:(i + 1) * TQ], oT.t[:, cc, :], reads=[oT.T],
                              writes=[Tscr[n][i]])
                        if tapname in tapd and s == 0 and l == 0:
                            tapdma(tapd[tapname][cc * 128:(cc + 1) * 128, i * TQ:(i + 1) * TQ], oT.t[:, cc, :], reads=[oT.T])

                if phase_end("P0b"):
                    return
                with ExitStack() as st:
                    rc = sb(st, "rc", [128, SEQ])
                    rsn = sb(st, "rsn", [128, SEQ])
                    S.dma(rc.t[:], D["rope_c"][:, :], writes=[rc.T])
                    S.dma(rsn.t[:], D["rope_s"][:, :], writes=[rsn.T])
                    msk = sb(st, "mskd", [128, 5, 512], BF16)
                    S.dma(msk.t[:].rearrange("p a b -> p (a b)"), D["masks"][:, 8 * 512:13 * 512], writes=[msk.T])
                    esk = sb(st, "esk", [128, 8])
                    S.dma(esk.t[:], D["swa_sinks"][l:l + 1, :].partition_broadcast(128), writes=[esk.T])
                    S.op("act", lambda: A.activation(out=esk.t[:], in_=esk.t[:], func=AF.Exp), [esk.T], [esk.T])
                    qd = [sb(st, "qd%d" % i, [128, SEQ], BF16) for i in range(4)]
                    kd = [sb(st, "kd%d" % i, [128, SEQ], BF16) for i in range(2)]
                    t12 = [(sb(st, "rt1_%d" % i, [128, 512]), sb(st, "rt2_%d" % i, [128, 512])) for i in range(2)]
                    ri = 0
                    for b in range(6):
                        wblk = load_wblock(D["win_fm"][l, b], 2048)
                        dst = qd[b] if b < 4 else kd[b - 4]
                        for i in range(NQT):
                            pa, pbk = proj_pair(wblk, i)
                            rope_evac(t12[ri % 2], pa, pbk, i, dst.t[:, i * TQ:(i + 1) * TQ], dst.T, rc, rsn)
                            ri += 1
                    pts = [sb(st, "ptd%d" % i, [128, 512], BF16) for i in range(3)]
                    pti = 0
                    odt = [sb(st, "odt%d" % i, [128, 4, 512], BF16) for i in range(2)]
                    oTs = [sb(st, "oTd%d" % i, [128, 4, 512], BF16) for i in range(2)]
                    dn = sb(st, "dnd", [128, 4])
                    rcp = sb(st, "rcd", [128, 4])
                    for i in range(NQT):
                        od = odt[i % 2]
                        for h in range(8):
                            hp, base, g = h // 2, (h % 2) * 64, h // 4
                            pv = PS[4 + h % 2]
                            pvv = pv.t[:, 0:260].rearrange("p (a b) -> p a b", a=4)
                            plan = []
                            for c in range(-1, 4):
                                kb = 4 * i + c
                                if kb < 0:
                                    continue
                                lo, hi = max(0, 128 * c), min(512, 128 * c + 256)
                                plan.append((c, kb, lo, hi))
                            nmm = sum((hi - lo) // 128 for (_, _, lo, hi) in plan)
                            k = 0
                            for (c, kb, lo, hi) in plan:
                                n = hi - lo
                                sc = next_ps()
                                mm(sc, sc.t[:, 0:n], kd[g].t[base:base + 64, kb * 128:(kb + 1) * 128],
                                   qd[hp].t[base:base + 64, i * TQ + lo:i * TQ + hi], True, False, [kd[g].T, qd[hp].T])
                                mm(sc, sc.t[:, 0:n], ident.t[:], msk.t[:, c + 1, lo:hi], False, True, [ident.T, msk.T])
                                pt = pts[pti % 3]
                                pti += 1
                                S.op("act", lambda: A.activation(out=pt.t[:, 0:n], in_=sc.t[:, 0:n], func=AF.Exp, scale=0.125),
                                     [sc.T], [pt.T])
                                for qs in range(lo // 128, hi // 128):
                                    mm(pv, pvv[:, qs, :], pt.t[:, qs * 128 - lo:qs * 128 - lo + 128],
                                       vall.t[:, kb, 4 + g, :], k == 0, k == nmm - 1, [pt.T, vall.T], skip=True)
                                    k += 1
                            S.op("dve", lambda: V.tensor_scalar(dn.t[:], pvv[:, :, 64], esk.t[:, h:h + 1], None, ALU.add),
                                 [pv.T, esk.T], [dn.T])
                            S.op("dve", lambda: V.reciprocal(rcp.t[:], dn.t[:]), [dn.T], [rcp.T])
                            S.op("dve", lambda: V.tensor_tensor(
                                od.t[:, :, h * 64:(h + 1) * 64], pvv[:, :, 0:64],
                                rcp.t[:].rearrange("p (a o) -> p a o", o=1).to_broadcast([128, 4, 64]), ALU.mult),
                                [pv.T, rcp.T], [od.T])
                        transpose_store(oTs[i % 2], od, od.T, 3, i, "od")
                    S.barrier()

                if phase_end("P1"):
                    return
                with ExitStack() as st:
                    cw = sb(st, "cw", [128, 4, 31])
                    S.dma(cw.t[:].rearrange("p a b -> p (a b)"), D["conv_w"][l], writes=[cw.T])
                    cprm = sb(st, "cprm", [128, 3, 4])
                    for j, nm in enumerate(("conv_b", "conv_ln_g", "conv_ln_b")):
                        S.dma(cprm.t[:, j, :], D[nm][l], writes=[cprm.T])
                    cbuf = sb(st, "cbuf", [128, 4, 30 + SEQ])
                    S.op("pool", lambda: P.memset(cbuf.t[:, :, 0:30], 0.0), [], [cbuf.T])
                    sg = [sb(st, "sg%d" % i, [128, 512]) for i in range(2)]
                    k = 0
                    for cc in range(4):
                        wblk = load_wblock(D["win_fm"][l, 6 + cc], 2048)
                        for i in range(NQT):
                            pa, pbk = proj_pair(wblk, i)
                            sgi = sg[k % 2]
                            k += 1
                            S.op("act", lambda: A.activation(out=sgi.t[:], in_=pbk.t[:], func=AF.Sigmoid), [pbk.T], [sgi.T])
                            S.op("dve", lambda: V.tensor_tensor(cbuf.t[:, cc, 30 + i * TQ:30 + (i + 1) * TQ], pa.t[:], sgi.t[:], ALU.mult),
                                 [pa.T, sgi.T], [cbuf.T])
                    conv = [sb(st, "conv%d" % i, [128, SEQ]) for i in range(4)]
                    accb = sb(st, "accb", [128, SEQ])
                    tmpb = sb(st, "tmpb", [128, SEQ])
                    for cc in range(4):
                        ca = conv[cc]
                        S.op("act", lambda: A.activation(out=ca.t[:], in_=cbuf.t[:, cc, 30:30 + SEQ], func=AF.Identity,
                                                         scale=cw.t[:, cc, 30:31], bias=cprm.t[:, 0, cc:cc + 1]),
                             [cbuf.T, cw.T, cprm.T], [ca.T])
                        for j in range(0, 20):
                            S.op("dve", lambda j=j: V.scalar_tensor_tensor(ca.t[:], cbuf.t[:, cc, j:j + SEQ], cw.t[:, cc, j:j + 1],
                                                                           ca.t[:], ALU.mult, ALU.add), [cbuf.T, cw.T, ca.T], [ca.T])
                        S.op("pool", lambda: P.tensor_scalar(accb.t[:], cbuf.t[:, cc, 20:20 + SEQ], cw.t[:, cc, 20:21], 0.0,
                                                             ALU.mult, ALU.add), [cbuf.T, cw.T], [accb.T])
                        for j in range(21, 30):
                            S.op("pool", lambda j=j: P.tensor_scalar(tmpb.t[:], cbuf.t[:, cc, j:j + SEQ], cw.t[:, cc, j:j + 1], 0.0,
                                                                     ALU.mult, ALU.add), [cbuf.T, cw.T], [tmpb.T])
                            S.op("pool", lambda: P.tensor_tensor(accb.t[:], accb.t[:], tmpb.t[:], ALU.add), [accb.T, tmpb.T], [accb.T])
                        S.op("pool", lambda: P.tensor_tensor(ca.t[:], ca.t[:], accb.t[:], ALU.add), [ca.T, accb.T], [ca.T])
                    sq = sb(st, "sq", [128, 4, 512])
                    mean = sb(st, "mean", [128, 512])
                    msq = sb(st, "msq", [128, 512])
                    var = sb(st, "var", [128, 512])
                    rstd = sb(st, "rstd", [128, 512])
                    dd = [sb(st, "dd%d" % i, [128, 512]) for i in range(2)]
                    obT = [sb(st, "obT%d" % i, [128, 4, 512], BF16) for i in range(2)]
                    for i in range(NQT):
                        sl = slice(i * TQ, (i + 1) * TQ)
                        S.op("act", lambda: A.activation(out=sq.t[:, 0, :], in_=conv[0].t[:, sl], func=AF.Square), [conv[0].T], [sq.T])
                        S.op("act", lambda: A.activation(out=sq.t[:, 1, :], in_=conv[1].t[:, sl], func=AF.Square), [conv[1].T], [sq.T])
                        S.op("act", lambda: A.activation(out=sq.t[:, 2, :], in_=conv[2].t[:, sl], func=AF.Square), [conv[2].T], [sq.T])
                        S.op("act", lambda: A.activation(out=sq.t[:, 3, :], in_=conv[3].t[:, sl], func=AF.Square), [conv[3].T], [sq.T])
                        p1 = next_ps()
                        p2 = next_ps()
                        for cc in range(4):
                            mm(p1, p1.t[:, :], ones32.t[:], conv[cc].t[:, sl], cc == 0, cc == 3, [ones32.T, conv[cc].T])
                        for cc in range(4):
                            mm(p2, p2.t[:, :], ones32.t[:], sq.t[:, cc, :], cc == 0, cc == 3, [ones32.T, sq.T])
                        S.op("dve", lambda: V.tensor_scalar(mean.t[:], p1.t[:], 1.0 / 512, None, ALU.mult), [p1.T], [mean.T])
                        S.op("dve", lambda: V.tensor_tensor(msq.t[:], mean.t[:], mean.t[:], ALU.mult), [mean.T], [msq.T])
                        S.op("dve", lambda: V.scalar_tensor_tensor(var.t[:], p2.t[:], 1.0 / 512, msq.t[:], ALU.mult, ALU.subtract),
                             [p2.T, msq.T], [var.T])
                        S.op("act", lambda: A.activation(out=var.t[:], in_=var.t[:], func=AF.Ln, bias=EPS), [var.T], [var.T])
                        S.op("act", lambda: A.activation(out=rstd.t[:], in_=var.t[:], func=AF.Exp, scale=-0.5), [var.T], [rstd.T])
                        ob = obT[i % 2]
                        for cc in range(4):
                            d = dd[cc % 2]
                            S.op("dve", lambda: V.tensor_tensor(d.t[:], conv[cc].t[:, sl], mean.t[:], ALU.subtract), [conv[cc].T, mean.T], [d.T])
                            S.op("dve", lambda: V.tensor_tensor(d.t[:], d.t[:], rstd.t[:], ALU.mult), [d.T, rstd.T], [d.T])
                            S.op("act", lambda: A.activation(out=ob.t[:, cc, :], in_=d.t[:], func=AF.Silu,
                                                             scale=cprm.t[:, 1, cc:cc + 1], bias=cprm.t[:, 2, cc:cc + 1]),
                                 [d.T, cprm.T], [ob.T])
                        for cc in range(4):
                            S.dma(scr_o[1, cc * 128:(cc + 1) * 128, sl], ob.t[:, cc, :], reads=[ob.T], writes=[Tscr[1][i]])
                            if "ob" in tapd and s == 0 and l == 0:
                                tapdma(tapd["ob"][cc * 128:(cc + 1) * 128, sl], ob.t[:, cc, :], reads=[ob.T])
                    S.barrier()

                if phase_end("P2"):
                    return
                with ExitStack() as st:
                    PADC = 16
                    ub = [sb(st, "ub%d" % i, [128, PADC + SEQ]) for i in range(4)]
                    pp = [sb(st, "pp%d" % i, [128, PADC + SEQ]) for i in range(2)]
                    for bfr in ub + pp:
                        S.op("pool", lambda bfr=bfr: P.memset(bfr.t[:, 0:PADC], 0.0), [], [bfr.T])
                    inv16 = sb(st, "inv16", [128, 16])
                    S.dma(inv16.t[:], D["inv16"][:, :], writes=[inv16.T])
                    pw = sb(st, "pw", [128, 4, 128], BF16)
                    load_cast(pw.t[:].rearrange("p a b -> p (a b)"), pw.T, D["pool_w"][l], 512)
                    psc = sb(st, "psc", [128, 4])
                    S.dma(psc.t[:], D["pool_scale"][l], writes=[psc.T])
                    for j in range(2):
                        wblk = load_wblock(D["win_fm"][l, 10 + j], 2048)
                        for i in range(NQT):
                            pa, pbk = proj_pair(wblk, i)
                            S.op("act", lambda: A.activation(out=ub[2 * j].t[:, PADC + i * TQ:PADC + (i + 1) * TQ], in_=pa.t[:], func=AF.Copy),
                                 [pa.T], [ub[2 * j].T])
                            S.op("dve", lambda: V.tensor_copy(ub[2 * j + 1].t[:, PADC + i * TQ:PADC + (i + 1) * TQ], pbk.t[:]),
                                 [pbk.T], [ub[2 * j + 1].T])
                    pooled = [sb(st, "pooled%d" % i, [128, SEQ], BF16) for i in range(2)]
                    tmpf = sb(st, "ptmp", [128, 16])
                    ocT = [sb(st, "ocT%d" % i, [128, SEQ], BF16) for i in range(2)]
                    for g in range(4):
                        W = 2 << g
                        src = ub[g]
                        for lev in range(g + 1):
                            sh = 1 << lev
                            dst = pp[lev % 2]
                            S.op("pool", lambda src=src, dst=dst, sh=sh: P.tensor_tensor(
                                dst.t[:, PADC:PADC + SEQ], src.t[:, PADC:PADC + SEQ], src.t[:, PADC - sh:PADC + SEQ - sh], ALU.add),
                                [src.T], [dst.T])
                            src = dst
                        pl = pooled[g % 2]
                        u = ub[g]
                        S.op("dve", lambda: V.scalar_tensor_tensor(pl.t[:], src.t[:, PADC:PADC + SEQ], 1.0 / W, u.t[:, PADC:PADC + SEQ],
                                                                   ALU.mult, ALU.subtract), [src.T, u.T], [pl.T])
                        S.op("dve", lambda: V.tensor_tensor(tmpf.t[:, 0:W - 1], src.t[:, PADC:PADC + W - 1], inv16.t[:, 0:W - 1], ALU.mult),
                             [src.T, inv16.T], [tmpf.T])
                        S.op("dve", lambda: V.tensor_tensor(pl.t[:, 0:W - 1], tmpf.t[:, 0:W - 1], u.t[:, PADC:PADC + W - 1], ALU.subtract),
                             [tmpf.T, u.T], [pl.T])
                        oc = ocT[g % 2]
                        for i in range(NQT):
                            ps = next_ps()
                            mm(ps, ps.t[:, :], pw.t[:, g, :], pl.t[:, i * TQ:(i + 1) * TQ], True, True, [pw.T, pl.T])
                            S.op("act", lambda: A.activation(out=oc.t[:, i * TQ:(i + 1) * TQ], in_=ps.t[:], func=AF.Copy,
                                                             scale=psc.t[:, g:g + 1]), [ps.T, psc.T], [oc.T])
                        S.dma(scr_o[2, g * 128:(g + 1) * 128, :], oc.t[:], reads=[oc.T], writes=Tscr[2])
                        if "oc" in tapd and s == 0 and l == 0:
                            tapdma(tapd["oc"][g * 128:(g + 1) * 128, :], oc.t[:], reads=[oc.T])
                    S.barrier()

                if phase_end("P3"):
                    return
                with ExitStack() as st:
                    msk = sb(st, "mska", [128, 8, 512], BF16)
                    S.dma(msk.t[:].rearrange("p a b -> p (a b)"), D["masks"][:, 0:8 * 512], writes=[msk.T])
                    mcmp = sb(st, "mcmp", [128, 4, 512], BF16)
                    S.dma(mcmp.t[:].rearrange("p a b -> p (a b)"), D["masks"][:, 13 * 512:17 * 512], writes=[mcmp.T])
                    selE = sb(st, "selE", [128, 16, 128], BF16)
                    S.dma(selE.t[:].rearrange("p a b -> p (a b)"), D["sel_e"][:, :], writes=[selE.T])
                    stab = sb(st, "stab", [128, 3, 8, 32])
                    S.dma(stab.t[:].rearrange("p a b c -> p (a b c)"), D["sel_tab"][:, :], writes=[stab.T])
                    qraw = [sb(st, "qraw%d" % i, [128, SEQ], BF16) for i in range(4)]
                    qrot = [sb(st, "qrot%d" % i, [128, SEQ], BF16) for i in range(4)]
                    ksl = [sb(st, "ksl%d" % i, [128, SEQ], BF16) for i in range(2)]
                    kwn = [sb(st, "kwn%d" % i, [128, SEQ], BF16) for i in range(2)]
                    kcT = sb(st, "kcT", [128, 2, 128], BF16)
                    vcx = sb(st, "vcx", [128, 2, 98], BF16)
                    ovl = sb(st, "ovl", [128, 32], BF16)
                    stc = st.enter_context(ExitStack())
                    csrc = [sb(stc, "csrc%d" % i, [128, SEQ], BF16) for i in range(2)]
                    strp = st.enter_context(ExitStack())
                    rc = sb(strp, "rc", [128, SEQ])
                    rsn = sb(strp, "rsn", [128, SEQ])
                    S.dma(rc.t[:], D["rope_c"][:, :], writes=[rc.T])
                    S.dma(rsn.t[:], D["rope_s"][:, :], writes=[rsn.T])
                    t12 = [(sb(strp, "rt1_%d" % i, [128, 512]), sb(strp, "rt2_%d" % i, [128, 512])) for i in range(2)]
                    ri = 0
                    if phase_end("P4s"):
                        return
                    for b in range(9):
                        wblk = load_wblock(D["win_fm"][l, 12 + b], 2048)
                        for i in range(NQT):
                            pa, pbk = proj_pair(wblk, i)
                            sl = slice(i * TQ, (i + 1) * TQ)
                            if b < 8:
                                dst = qrot[b] if b < 4 else (ksl[b - 4] if b < 6 else kwn[b - 6])
                                if b < 4:
                                    S.op("act", lambda: A.activation(out=qraw[b].t[:, sl], in_=pa.t[:], func=AF.Copy), [pa.T], [qraw[b].T])
                                rope_evac(t12[ri % 2], pa, pbk, i, dst.t[:, sl], dst.T, rc, rsn)
                                ri += 1
                            else:
                                S.op("act", lambda: A.activation(out=csrc[0].t[:, sl], in_=pa.t[:], func=AF.Copy), [pa.T], [csrc[0].T])
                                S.op("dve", lambda: V.tensor_copy(csrc[1].t[:, sl], pbk.t[:]), [pbk.T], [csrc[1].T])
                    S.barrier()
                    strp.close()
                    if phase_end("P4a"):
                        return
                    S.dma(ovl.t[:], D["overlap"][:, :], writes=[ovl.T])
                    S.op("pool", lambda: P.memset(vcx.t[:], 0.0), [], [vcx.T])
                    S.op("pool", lambda: P.memset(vcx.t[:, :, 64:65], 1.0), [], [vcx.T])
                    for g in range(2):
                        S.op("dve", lambda g=g: V.tensor_copy(vcx.t[:, g, 66:98], ovl.t[:]), [ovl.T], [vcx.T])
                    with ExitStack() as st2:
                        w1 = sb(st2, "w1", [128, 32, 256], BF16)
                        kblk = sb(st2, "kblk", [128, 32, 128], BF16)
                        posb = sb(st2, "posb", [128, 32])
                        hid = sb(st2, "hid", [128, 2, 2, 128], BF16)
                        w2k = sb(st2, "w2k", [128, 2, 128], BF16)
                        w2v = sb(st2, "w2v", [128, 2, 64], BF16)
                        load_cast(w2k.t[:].rearrange("p a b -> p (a b)"), w2k.T, D["cmp_w2k"][l], 256)
                        load_cast(w2v.t[:].rearrange("p a b -> p (a b)"), w2v.T, D["cmp_w2v"][l], 128)
                        for kv in range(2):
                            for pc in range(4):
                                load_cast(w1.t[:, pc * 8:(pc + 1) * 8, :].rearrange("p a b -> p (a b)"), w1.T,
                                          D["cmp_w1"][l, kv, pc], 2048)
                            S.dma(posb.t[:], D["cmp_pos"][l, kv], writes=[posb.T])
                            srcv = csrc[kv].t[:, 0:SEQ].rearrange("p (n s) -> p s n", s=16)
                            for hl in range(2):
                                S.op("dve", lambda hl=hl: V.tensor_tensor(
                                    kblk.t[:, hl * 16:(hl + 1) * 16, 0:127], srcv[:, :, hl:hl + 127],
                                    posb.t[:, hl * 16:(hl + 1) * 16].rearrange("p (a o) -> p a o", o=1).to_broadcast([128, 16, 127]),
                                    ALU.add), [csrc[kv].T, posb.T], [kblk.T])
                            for hc in range(2):
                                for g in range(2):
                                    ps = next_ps()
                                    for ll in range(32):
                                        mm(ps, ps.t[:, 0:127], w1.t[g * 64:(g + 1) * 64, ll, hc * 128:(hc + 1) * 128],
                                           kblk.t[g * 64:(g + 1) * 64, ll, 0:127], ll == 0, ll == 31, [w1.T, kblk.T])
                                    S.op("act", lambda g=g: A.activation(out=hid.t[:, hc, g, 0:127], in_=ps.t[:, 0:127],
                                                                         func=AF.Gelu_apprx_tanh), [ps.T], [hid.T])
                            if kv == 0:
                                ps = next_ps()
                                for g in range(2):
                                    for hc in range(2):
                                        mm(ps, ps.t[:, g * 128:g * 128 + 127], w2k.t[:, hc, :], hid.t[:, hc, g, 0:127],
                                           (g == 0 and hc == 0), (g == 1 and hc == 1), [w2k.T, hid.T], skip=True)
                                for g in range(2):
                                    S.op("dve", lambda g=g: V.tensor_copy(kcT.t[:, g, 0:127], ps.t[:, g * 128:g * 128 + 127]), [ps.T], [kcT.T])
                            else:
                                for g in range(2):
                                    ps = next_ps()
                                    for hc in range(2):
                                        mm(ps, ps.t[0:127, 0:64], hid.t[:, hc, g, 0:127], w2v.t[:, hc, :], hc == 0, hc == 1,
                                           [w2v.T, hid.T])
                                    S.op("dve", lambda g=g: V.tensor_copy(vcx.t[0:127, g, 0:64], ps.t[0:127, 0:64]), [ps.T], [vcx.T])
                        S.barrier()
                    stc.close()
                    if phase_end("P4b"):
                        return
                    pts = [sb(st, "pta%d" % i, [128, 512], BF16) for i in range(3)]
                    pti = [0]
                    oacc = sb(st, "oacc", [128, 4, 512])
                    oabf = [sb(st, "oabf%d" % i, [128, 4, 512], BF16) for i in range(2)]
                    oTs = [sb(st, "oTa%d" % i, [128, 4, 512], BF16) for i in range(2)]
                    imp = sb(st, "imp", [128, 4, 2, 32])
                    dn = sb(st, "dna", [128, 4])
                    rcp = sb(st, "rca", [128, 4])
                    ff = sb(st, "ffa", [128, 4])
                    tmpo = sb(st, "tmpo", [128, 4, 64])
                    tmpi = sb(st, "tmpi", [128, 4, 32])
                    score = sb(st, "score", [128, 4, 32])
                    mx8 = sb(st, "mx8", [128, 16])
                    mrep = sb(st, "mrep", [128, 32])
                    sel = sb(st, "sel", [128, 32])
                    selb = sb(st, "selb", [128, 32], BF16)
                    selbT = [sb(st, "selbT%d" % i, [128, 512], BF16) for i in range(2)]
                    for g in range(2):
                        S.op("pool", lambda g=g: P.memset(selbT[g].t[:], 0.0), [], [selbT[g].T])

                    def finish_head(pv, pvv, i, h, br, first, guard):
                        if guard:
                            S.op("dve", lambda: V.tensor_scalar(dn.t[:], pvv[:, :, 64], 1e-30, None, ALU.max), [pv.T], [dn.T])
                            S.op("dve", lambda: V.reciprocal(rcp.t[:], dn.t[:]), [dn.T], [rcp.T])
                        else:
                            S.op("dve", lambda: V.reciprocal(rcp.t[:], pvv[:, :, 64]), [pv.T], [rcp.T])
                        S.op("dve", lambda: V.tensor_tensor(ff.t[:], rcp.t[:], ga.t[:, i * 4:(i + 1) * 4, br * 8 + h], ALU.mult),
                             [rcp.T, ga.T], [ff.T])
                        fb = ff.t[:].rearrange("p (a o) -> p a o", o=1).to_broadcast([128, 4, 64])
                        if first:
                            S.op("dve", lambda: V.tensor_tensor(oacc.t[:, :, h * 64:(h + 1) * 64], pvv[:, :, 0:64], fb, ALU.mult),
                                 [pv.T, ff.T], [oacc.T])
                        else:
                            S.op("dve", lambda: V.tensor_tensor(tmpo.t[:], pvv[:, :, 0:64], fb, ALU.mult), [pv.T, ff.T], [tmpo.T])
                            S.op("pool", lambda: P.tensor_tensor(oacc.t[:, :, h * 64:(h + 1) * 64], oacc.t[:, :, h * 64:(h + 1) * 64],
                                                                 tmpo.t[:], ALU.add), [oacc.T, tmpo.T], [oacc.T])

                    def score_block(sc, n, k_ap, kT_, q_ap, qT_, extra):
                        mm(sc, sc.t[:, 0:n], k_ap, q_ap, True, len(extra) == 0, [kT_, qT_])
                        for e_i, (l_ap, lT, r_ap, rT) in enumerate(extra):
                            mm(sc, sc.t[:, 0:n], l_ap, r_ap, False, e_i == len(extra) - 1, [lT, rT])
                        pt = pts[pti[0] % 3]
                        pti[0] += 1
                        S.op("act", lambda: A.activation(out=pt.t[:, 0:n], in_=sc.t[:, 0:n], func=AF.Exp, scale=0.125), [sc.T], [pt.T])
                        return pt

                    for i in range(NQT):
                        qsl = slice(i * TQ, (i + 1) * TQ)
                        for h in range(8):
                            hp, base, g = h // 2, (h % 2) * 64, h // 4
                            pv = PS[4 + h % 2]
                            pvv = pv.t[:, 0:392].rearrange("p (a b) -> p a b", a=4)
                            sc = next_ps()
                            mm(sc, sc.t[0:127, :], kcT.t[base:base + 64, g, 0:127], qraw[hp].t[base:base + 64, qsl], True, False,
                               [kcT.T, qraw[hp].T])
                            mm(sc, sc.t[0:127, :], ident.t[0:127, 0:127], mcmp.t[0:127, i, :], False, True, [ident.T, mcmp.T])
                            pt = pts[pti[0] % 3]
                            pti[0] += 1
                            S.op("act", lambda: A.activation(out=pt.t[0:127, :], in_=sc.t[0:127, :], func=AF.Exp, scale=0.125), [sc.T], [pt.T])
                            for qs in range(4):
                                mm(pv, pvv[:, qs, :], pt.t[0:127, qs * 128:(qs + 1) * 128], vcx.t[0:127, g, :], qs == 0, qs == 3,
                                   [pt.T, vcx.T], skip=True)
                            finish_head(pv, pvv, i, h, 0, True, True)
                            if i >= 2:
                                rb = rcp.t[:].rearrange("p (a o) -> p a o", o=1).to_broadcast([128, 4, 32])
                                if h % 4 == 0:
                                    S.op("dve", lambda: V.tensor_tensor(imp.t[:, :, g, :], pvv[:, :, 66:98], rb, ALU.mult), [pv.T, rcp.T], [imp.T])
                                else:
                                    S.op("dve", lambda: V.tensor_tensor(tmpi.t[:], pvv[:, :, 66:98], rb, ALU.mult), [pv.T, rcp.T], [tmpi.T])
                                    S.op("dve", lambda: V.tensor_tensor(imp.t[:, :, g, :], imp.t[:, :, g, :], tmpi.t[:], ALU.add), [imp.T, tmpi.T], [imp.T])
                        if i >= 2:
                            tsl = slice((i - 2) * 4, (i - 2) * 4 + 4)
                            for g in range(2):
                                S.op("dve", lambda: V.tensor_tensor(score.t[:], imp.t[:, :, g, :], stab.t[:, 0, tsl, :], ALU.add), [imp.T, stab.T], [score.T])
                                S.op("dve", lambda: V.tensor_tensor(score.t[:], score.t[:], stab.t[:, 1, tsl, :], ALU.mult), [score.T, stab.T], [score.T])
                                S.op("dve", lambda: V.tensor_tensor(score.t[:], score.t[:], stab.t[:, 2, tsl, :], ALU.add), [score.T, stab.T], [score.T])
                                for qs in range(4):
                                    S.op("dve", lambda: V.max(out=mx8.t[:, 0:8], in_=score.t[:, qs, :]), [score.T], [mx8.T])
                                    S.op("dve", lambda: V.match_replace(out=mrep.t[:], in_to_replace=mx8.t[:, 0:8], in_values=score.t[:, qs, :],
                                                                        imm_value=-1e9), [score.T, mx8.T], [mrep.T])
                                    S.op("dve", lambda: V.max(out=mx8.t[:, 8:16], in_=mrep.t[:]), [mrep.T], [mx8.T])
                                    S.op("dve", lambda: V.tensor_scalar(sel.t[:], score.t[:, qs, :], mx8.t[:, 15:16], None, ALU.is_ge), [score.T, mx8.T], [sel.T])
                                    S.op("dve", lambda: V.tensor_scalar(selb.t[:], sel.t[:], -NEGM, NEGM, ALU.mult, ALU.add), [sel.T], [selb.T])
                                    pb = next_pb()
                                    S.op("pe", lambda: PE.transpose(pb.t[0:32, 0:128], selb.t[:, :], ident.t[:]), [selb.T, ident.T], [pb.T])
                                    S.op("act", lambda: A.activation(out=selbT[g].t[0:32, qs * 128:(qs + 1) * 128], in_=pb.t[0:32, 0:128], func=AF.Copy),
                                         [pb.T], [selbT[g].T])
                        for h in range(8):
                            hp, base, g = h // 2, (h % 2) * 64, h // 4
                            pv = PS[4 + h % 2]
                            pvv = pv.t[:, 0:260].rearrange("p (a b) -> p a b", a=4)
                            plan = []
                            for kb in range(0, 4 * i + 4):
                                c = kb - 4 * i
                                lo = max(0, 128 * c)
                                plan.append((kb, c, lo))
                            nmm = sum((512 - lo) // 128 for (_, _, lo) in plan)
                            k = 0
                            for (kb, c, lo) in plan:
                                n = 512 - lo
                                extra = []
                                if i >= 2:
                                    extra.append((selE.t[:, kb, :], selE.T, selbT[g].t[:, lo:512], selbT[g].T))
                                if c >= 0:
                                    extra.append((ident.t[:], ident.T, msk.t[:, c, lo:512], msk.T))
                                sc = next_ps()
                                pt = score_block(sc, n, ksl[g].t[base:base + 64, kb * 128:(kb + 1) * 128], ksl[g].T,
                                                 qrot[hp].t[base:base + 64, i * TQ + lo:(i + 1) * TQ], qrot[hp].T, extra)
                                for qs in range(lo // 128, 4):
                                    mm(pv, pvv[:, qs, :], pt.t[:, qs * 128 - lo:qs * 128 - lo + 128], vall.t[:, kb, 0 + g, :],
                                       k == 0, k == nmm - 1, [pt.T, vall.T], skip=True)
                                    k += 1
                            finish_head(pv, pvv, i, h, 1, False, False)
                        for h in range(8):
                            hp, base, g = h // 2, (h % 2) * 64, h // 4
                            pv = PS[4 + h % 2]
                            pvv = pv.t[:, 0:260].rearrange("p (a b) -> p a b", a=4)
                            plan = []
                            for kb in range(max(0, 4 * i - 4), 4 * i + 4):
                                c = kb - 4 * i
                                if c >= 0:
                                    plan.append((kb, msk.t[:, c, 128 * c:512], 128 * c, 512))
                                else:
                                    cp = c + 4
                                    plan.append((kb, msk.t[:, 4 + cp, 0:128 * (cp + 1)], 0, 128 * (cp + 1)))
                            nmm = sum((hi - lo) // 128 for (_, _, lo, hi) in plan)
                            k = 0
                            for (kb, m_ap, lo, hi) in plan:
                                n = hi - lo
                                sc = next_ps()
                                pt = score_block(sc, n, kwn[g].t[base:base + 64, kb * 128:(kb + 1) * 128], kwn[g].T,
                                                 qrot[hp].t[base:base + 64, i * TQ + lo:i * TQ + hi], qrot[hp].T,
                                                 [(ident.t[:], ident.T, m_ap, msk.T)])
                                for qs in range(lo // 128, hi // 128):
                                    mm(pv, pvv[:, qs, :], pt.t[:, qs * 128 - lo:qs * 128 - lo + 128], vall.t[:, kb, 2 + g, :],
                                       k == 0, k == nmm - 1, [pt.T, vall.T], skip=True)
                                    k += 1
                            finish_head(pv, pvv, i, h, 2, False, False)
                        ob = oabf[i % 2]
                        S.op("act", lambda: A.activation(out=ob.t[:], in_=oacc.t[:], func=AF.Copy), [oacc.T], [ob.T])
                        transpose_store(oTs[i % 2], ob, ob.T, 0, i, "oa")
                        if phase_end("P4q%d" % i):
                            return
                    S.barrier()

                if phase_end("P4"):
                    return
                av.close()
                with ExitStack() as st:
                    oall = sb(st, "oall", [128, 4, 4, SEQ], BF16)
                    oallT = [T("oall%d" % n) for n in range(4)]
                    for n in range(4):
                        for cc in range(4):
                            S.dma(oall.t[:, n, cc, :], scr_o[n, cc * 128:(cc + 1) * 128, :], reads=Tscr[n], writes=[oallT[n]])
                    mixT = sb(st, "mixT", [128, 8, SEQ], BF16)
                    stm = st.enter_context(ExitStack())
                    acc = [sb(stm, "macc%d" % i, [128, 512]) for i in range(NQT)]
                    sgm = [sb(stm, "msg%d" % i, [128, 512]) for i in range(2)]
                    prd = [sb(stm, "mprd%d" % i, [128, 512]) for i in range(2)]
                    k = 0
                    for dc in range(8):
                        for n in range(4):
                            wblk = load_wblock(D["w_merge"][l, dc, n], 1536)
                            for i in range(NQT):
                                sl = slice(i * TQ, (i + 1) * TQ)
                                pu = next_ps(0, 6)
                                pg = next_ps(0, 6)
                                for cc in range(4):
                                    mm(pu, pu.t[:, :], wblk.t[:, cc * 128:(cc + 1) * 128], oall.t[:, n, cc, sl], cc == 0, cc == 3,
                                       [wblk.T, oallT[n]])
                                for kc in range(8):
                                    mm(pg, pg.t[:, :], wblk.t[:, 512 + kc * 128:512 + (kc + 1) * 128], hT.t[:, kc, sl], kc == 0, kc == 7,
                                       [wblk.T, hTT[i]])
                                sg = sgm[k % 2]
                                pr = prd[k % 2]
                                k += 1
                                S.op("act", lambda: A.activation(out=sg.t[:], in_=pg.t[:], func=AF.Sigmoid), [pg.T], [sg.T])
                                if n == 0:
                                    S.op("dve", lambda: V.tensor_tensor(acc[i].t[:], sg.t[:], pu.t[:], ALU.mult), [sg.T, pu.T], [acc[i].T])
                                else:
                                    S.op("dve", lambda: V.tensor_tensor(pr.t[:], sg.t[:], pu.t[:], ALU.mult), [sg.T, pu.T], [pr.T])
                                    if n < 3:
                                        S.op("pool", lambda: P.tensor_tensor(acc[i].t[:], acc[i].t[:], pr.t[:], ALU.add), [acc[i].T, pr.T], [acc[i].T])
                                    else:
                                        S.op("pool", lambda: P.tensor_tensor(mixT.t[:, dc, sl], acc[i].t[:], pr.t[:], ALU.add),
                                             [acc[i].T, pr.T], [mixT.T])
                    S.barrier()
                    stm.close()
                    wo = sb(st, "wo", [128, 8, DM], BF16)
                    for pc in range(4):
                        load_cast(wo.t[:, pc * 2:(pc + 1) * 2, :].rearrange("p a b -> p (a b)"), wo.T, D["w_o"][l, pc], 2048)

                    def get_ps(tt):
                        pa = next_ps(0, 6)
                        pbk = next_ps(0, 6)
                        for hf, pp_ in enumerate((pa, pbk)):
                            for kc in range(8):
                                mm(pp_, pp_.t[:, :], mixT.t[:, kc, tt * 128:(tt + 1) * 128], wo.t[:, kc, hf * 512:(hf + 1) * 512],
                                   kc == 0, kc == 7, [mixT.T, wo.T])
                        return pa, pbk

                    out_epilogue(st, s, l, "norm_mix_post", get_ps, "mo")
                    S.barrier()
            if "x1" in tapd and s == 0 and l == 0:
                xt_ = None

        def ffn(s, l):
            with ExitStack() as fx:
                actT = sb(fx, "actT", [128, NFC, SEQ], BF16)
                with ExitStack() as st:
                    hfT = sb(st, "hfT", [128, 8, 2 + SEQ], BF16)
                    hfTT = [T("hfT%d" % i) for i in range(NQT)]
                    hall = T("hfTall")
                    S.op("pool", lambda: P.memset(hfT.t[:, :, 0:2], 0.0), [], hfTT)
                    with ExitStack() as st2:
                        norm_transpose(st2, s, l, lambda tt: y[s, tt * 128:(tt + 1) * 128, :], lambda tt: [Ty[s][tt]],
                                       "norm_ffn_pre", hfT, hfTT, 2)
                        S.barrier()
                    fcw = sb(st, "fcw", [128, 44, 3])
                    fcb = sb(st, "fcb", [128, 44])
                    S.dma(fcw.t[:].rearrange("p a b -> p (a b)"), D["ffn_cw"][l], writes=[fcw.T])
                    S.dma(fcb.t[:], D["ffn_cb"][l], writes=[fcb.T])
                    yg = [sb(st, "yg%d" % i, [128, 512]) for i in range(2)]
                    yv = [sb(st, "yv%d" % i, [128, 512]) for i in range(2)]
                    gl = [sb(st, "gl%d" % i, [128, 512]) for i in range(2)]
                    k = 0
                    for c in range(NFC):
                        wblk = load_wblock(D["w_up"][l, c], 2048)
                        for (st0, n) in FFN_TILES:
                            pg = next_ps(0, 6)
                            pv = next_ps(0, 6)
                            for half, pp_ in enumerate((pg, pv)):
                                for kc in range(8):
                                    mm(pp_, pp_.t[:, 0:n + 2], wblk.t[:, kc * 256 + half * 128:kc * 256 + half * 128 + 128],
                                       hfT.t[:, kc, st0:st0 + n + 2], kc == 0, kc == 7, hfTT + [wblk.T])
                            ygb, yvb, glb = yg[k % 2], yv[k % 2], gl[k % 2]
                            k += 1
                            for (pp_, yb, ch) in ((pg, ygb, c), (pv, yvb, NFC + c)):
                                S.op("act", lambda pp_=pp_, yb=yb, ch=ch: A.activation(
                                    out=yb.t[:, 0:n], in_=pp_.t[:, 2:n + 2], func=AF.Identity, scale=fcw.t[:, ch, 2:3], bias=fcb.t[:, ch:ch + 1]),
                                    [pp_.T, fcw.T, fcb.T], [yb.T])
                                S.op("dve", lambda pp_=pp_, yb=yb, ch=ch: V.scalar_tensor_tensor(
                                    yb.t[:, 0:n], pp_.t[:, 1:n + 1], fcw.t[:, ch, 1:2], yb.t[:, 0:n], ALU.mult, ALU.add),
                                    [pp_.T, fcw.T, yb.T], [yb.T])
                                S.op("dve", lambda pp_=pp_, yb=yb, ch=ch: V.scalar_tensor_tensor(
                                    yb.t[:, 0:n], pp_.t[:, 0:n], fcw.t[:, ch, 0:1], yb.t[:, 0:n], ALU.mult, ALU.add),
                                    [pp_.T, fcw.T, yb.T], [yb.T])
                            S.op("act", lambda: A.activation(out=glb.t[:, 0:n], in_=ygb.t[:, 0:n], func=AF.Gelu_apprx_tanh), [ygb.T], [glb.T])
                            S.op("pool", lambda: P.tensor_tensor(actT.t[:, c, st0:st0 + n], glb.t[:, 0:n], yvb.t[:, 0:n], ALU.mult),
                                 [glb.T, yvb.T], [actT.T])
                    S.barrier()
                with ExitStack() as st:
                    wd = sb(st, "wd", [128, NFC, DM], BF16)
                    for pc in range(11):
                        load_cast(wd.t[:, pc * 2:(pc + 1) * 2, :].rearrange("p a b -> p (a b)"), wd.T, D["w_down"][l, pc], 2048)

                    def get_ps(tt):
                        pa = next_ps(0, 6)
                        pbk = next_ps(0, 6)
                        for hf, pp_ in enumerate((pa, pbk)):
                            for kc in range(NFC):
                                mm(pp_, pp_.t[:, :], actT.t[:, kc, tt * 128:(tt + 1) * 128], wd.t[:, kc, hf * 512:(hf + 1) * 512],
                                   kc == 0, kc == NFC - 1, [actT.T, wd.T])
                        return pa, pbk

                    out_epilogue(st, s, l, "norm_ffn_post", get_ps, "fo")
                    S.barrier()

        for s in range(nseq):
            for l in range(nl):
                if not stopped[0]:
                    mixer(s, l)
                if not stopped[0] and not phase_end("P5"):
                    ffn(s, l)
                    if not (s == nseq - 1 and l == nl - 1):
                        S.rotate()
        S.finish(fin)
        print("program instructions:", S.ninst)
    return nc


_NP2BIR = {np.dtype(np.float32): F32, np.dtype(ml_dtypes.bfloat16): BF16}


def run(inputs, nseq_per_core=2, ncores=8, nl=2, taps=(), stop_after=None):
    w = prep_weights(inputs)
    wshapes = {k: (v.shape, _NP2BIR[v.dtype]) for k, v in w.items()}
    nc = build_program(nseq_per_core, nl, wshapes, taps, stop_after)
    x = np.ascontiguousarray(np.asarray(inputs["x"], np.float32))
    in_maps = []
    for c in range(ncores):
        m = dict(w)
        m["x"] = np.ascontiguousarray(x[c * nseq_per_core:(c + 1) * nseq_per_core])
        in_maps.append(m)
    res = run_bass_kernel_spmd(nc, in_maps, core_ids=list(range(ncores)))
    return res.results


def kernel(**inputs):
    results = run(inputs)
    return np.concatenate([r["y"] for r in results], axis=0).astype(np.float32)
```

```python
import numpy as np
from contextlib import ExitStack
import ml_dtypes
import concourse.bass as bass
import concourse.mybir as mybir
from concourse.bass_utils import run_bass_kernel_spmd

F32 = mybir.dt.float32
BF16 = mybir.dt.bfloat16
AF = mybir.ActivationFunctionType
ALU = mybir.AluOpType

SEQ = 2048
DM = 1024
NQT = 4
TQ = 512
NTT = 16
EPS = 1e-6
NEGM = -30000.0
DFF = 2816
NFC = 22
OFF_AQ, OFF_AKV, OFF_AG, OFF_DQ, OFF_DKV, OFF_B, OFF_C = 0, 512, 1280, 1304, 1816, 2072, 3096
NFM = 21
NTM = 408
FFN_TILES = [(0, 510), (510, 510), (1020, 510), (1530, 510), (2040, 8)]


class T:
    __slots__ = ("name", "w", "r", "excl")

    def __init__(self, name="", excl=False):
        self.name = name
        self.w = None
        self.r = {}
        self.excl = excl


class B:
    __slots__ = ("t", "T")

    def __init__(self, t, name="", excl=False):
        self.t = t
        self.T = T(name, excl)


class Sched:
    NDMA = 24

    def __init__(self, nc, es):
        self.nc = nc
        self.es = es
        self.engs = {"pe": nc.tensor, "dve": nc.vector, "act": nc.scalar,
                     "pool": nc.gpsimd, "sp": nc.sync}
        self.semobj = {}
        self.gen = 0
        self.key = {}
        self.cnt = {}
        for k in self.engs:
            self.key[k] = "%s@0" % k
            self.semobj[self.key[k]] = es.enter_context(nc.semaphore("s_%s_0" % k))
            self.cnt[k] = 0
        self.dsem = [es.enter_context(nc.semaphore("d%d" % i)) for i in range(self.NDMA)]
        self.dval = [0] * self.NDMA
        self.dnext = 0
        self.known = {k: {} for k in self.engs}
        for i, sm in enumerate(self.dsem):
            self.semobj["d%d" % i] = sm
        self.ninst = 0

    def rotate(self):
        self.barrier()
        self.gen += 1
        for k in self.engs:
            self.key[k] = "%s@%d" % (k, self.gen)
            self.semobj[self.key[k]] = self.es.enter_context(self.nc.semaphore("s_%s_%d" % (k, self.gen)))
            self.cnt[k] = 0

    def _wait(self, e, tok):
        if tok is None:
            return
        key, val = tok
        if e == "pe" and key.startswith("pe@"):
            return
        if self.known[e].get(key, 0) >= val:
            return
        self.engs[e].wait_ge(self.semobj[key], val)
        self.known[e][key] = val
        self.ninst += 1

    def _deps(self, e, reads, writes):
        for t in reads:
            self._wait(e, t.w)
        for t in writes:
            self._wait(e, t.w)
            for k, v in t.r.items():
                self._wait(e, (k, v))

    def _commit(self, tok, reads, writes):
        for t in reads:
            if t.r.get(tok[0], 0) < tok[1]:
                t.r[tok[0]] = tok[1]
        for t in writes:
            t.w = tok
            t.r = {}

    def op(self, e, fn, reads=(), writes=()):
        if any(t.excl for t in reads):
            writes = list(writes) + [t for t in reads if t.excl]
            reads = [t for t in reads if not t.excl]
        self._deps(e, reads, writes)
        ins = fn()
        self.cnt[e] += 1
        ins.then_inc(self.semobj[self.key[e]], 1)
        tok = (self.key[e], self.cnt[e])
        self.known[e][self.key[e]] = 0 if self.known[e].get(self.key[e]) is None else self.known[e][self.key[e]]
        self._commit(tok, reads, writes)
        self.ninst += 1
        return tok

    def dma(self, out, in_, reads=(), writes=(), q="sp"):
        i = self.dnext
        self.dnext = (self.dnext + 1) % self.NDMA
        key = "d%d" % i
        if self.dval[i] > 0:
            self._wait(q, (key, self.dval[i]))
        self._deps(q, reads, writes)
        ins = self.engs[q].dma_start(out=out, in_=in_)
        self.dval[i] += 16
        ins.then_inc(self.dsem[i], 16)
        tok = (key, self.dval[i])
        self._commit(tok, reads, writes)
        self.ninst += 1
        return tok

    def barrier(self):
        toks = [(self.key[k], self.cnt[k]) for k in self.engs if self.cnt[k] > 0]
        toks += [("d%d" % i, self.dval[i]) for i in range(self.NDMA) if self.dval[i] > 0]
        for e in self.engs:
            for tok in toks:
                if tok[0] != self.key[e]:
                    self._wait(e, tok)

    def finish(self, toks):
        for tok in toks:
            self._wait("sp", tok)


def _swap64(c):
    return np.concatenate([c[32:], c[:32]])


def _fm_blocks():
    ar = np.arange(128)
    a64 = np.arange(64)
    blocks = []

    def rope_pair(A):
        return (A, np.concatenate([_swap64(A[:64]), _swap64(A[64:])]))

    def dup_pair(k):
        return (np.concatenate([k, k]), np.concatenate([_swap64(k), _swap64(k)]))

    for hp in range(4):
        blocks.append(rope_pair(OFF_DQ + hp * 128 + ar))
    for g in range(2):
        blocks.append(dup_pair(OFF_DKV + g * 64 + a64))
    for cc in range(4):
        blocks.append((OFF_B + cc * 128 + ar, OFF_B + 512 + cc * 128 + ar))
    for j in range(2):
        blocks.append((OFF_C + (2 * j) * 128 + ar, OFF_C + (2 * j + 1) * 128 + ar))
    for hp in range(4):
        blocks.append(rope_pair(OFF_AQ + hp * 128 + ar))
    for g in range(2):
        blocks.append(dup_pair(OFF_AKV + ((1 * 2 + 0) * 2 + g) * 64 + a64))
    for g in range(2):
        blocks.append(dup_pair(OFF_AKV + ((2 * 2 + 0) * 2 + g) * 64 + a64))
    blocks.append((OFF_AKV + ar, OFF_AKV + 128 + ar))
    assert len(blocks) == NFM
    return blocks


def _tm_cols():
    ar = np.arange(128)
    return np.concatenate([OFF_AKV + 384 + ar, OFF_AKV + 640 + ar, OFF_DKV + 128 + ar,
                           OFF_AG + np.arange(24)])


def _kmajor(w, ncols):
    kc = w.shape[0] // 128
    return np.ascontiguousarray(w.reshape(kc, 128, ncols).transpose(1, 0, 2).reshape(128, kc * ncols))


def _bf16(a):
    return np.asarray(a, dtype=np.float32).astype(ml_dtypes.bfloat16)


def prep_weights(inp):
    L = inp["w_in"].shape[0]
    f32 = np.float32
    out = {}
    blocks = _fm_blocks()
    tmc = _tm_cols()
    win_fm = np.empty((L, NFM, 128, 2048), f32)
    win_tm = np.empty((L, 128, 8 * NTM), f32)
    for l in range(L):
        w = np.asarray(inp["w_in"][l], f32)
        for b, (A, Bc) in enumerate(blocks):
            win_fm[l, b] = _kmajor(w[:, np.concatenate([A, Bc])], 256)
        win_tm[l] = _kmajor(w[:, tmc], NTM)
    out["win_fm"] = win_fm
    out["win_tm"] = win_tm
    for nm in ("norm_mix_pre", "norm_mix_post", "norm_ffn_pre", "norm_ffn_post"):
        out[nm] = np.ascontiguousarray(np.asarray(inp[nm], f32))
    pos = np.asarray(inp["nsa_cmp_pos"], f32)
    posT = pos.transpose(0, 1, 3, 2)
    out["cmp_pos"] = np.ascontiguousarray(np.concatenate([posT, posT], axis=2))
    w1 = np.asarray(inp["nsa_cmp_w1"], f32).reshape(L, 2, 32, 64, 256).transpose(0, 1, 3, 2, 4)
    w1 = np.concatenate([w1, w1], axis=2)
    out["cmp_w1"] = np.ascontiguousarray(w1.reshape(L, 2, 128, 4, 8 * 256).transpose(0, 1, 3, 2, 4))
    w2 = np.asarray(inp["nsa_cmp_w2"], f32)
    w2k = np.concatenate([w2[:, 0], w2[:, 0]], axis=2)
    out["cmp_w2k"] = np.stack([_kmajor(w2k[l], 128) for l in range(L)])
    out["cmp_w2v"] = np.stack([_kmajor(w2[l, 1], 64) for l in range(L)])
    out["swa_sinks"] = np.ascontiguousarray(np.asarray(inp["swa_sinks"], f32))
    cw = np.asarray(inp["conv_w"], f32)
    out["conv_w"] = np.ascontiguousarray(cw.reshape(L, 31, 4, 128).transpose(0, 3, 2, 1).reshape(L, 128, 124))
    for nm in ("conv_b", "conv_ln_g", "conv_ln_b", "pool_scale"):
        out[nm] = np.ascontiguousarray(np.asarray(inp[nm], f32).reshape(L, 4, 128).transpose(0, 2, 1))
    out["pool_w"] = np.ascontiguousarray(np.asarray(inp["pool_w"], f32).transpose(0, 2, 1, 3).reshape(L, 128, 512))
    wb = np.asarray(inp["w_branch"], f32)
    wg = np.asarray(inp["w_gate"], f32)
    wm = np.empty((L, 8, 4, 128, 1536), f32)
    for l in range(L):
        for dc in range(8):
            for n in range(4):
                wm[l, dc, n, :, 0:512] = _kmajor(wb[l, n][:, dc * 128:(dc + 1) * 128], 128)
                wm[l, dc, n, :, 512:1536] = _kmajor(wg[l][:, n * 1024 + dc * 128: n * 1024 + (dc + 1) * 128], 128)
    out["w_merge"] = wm
    out["w_o"] = np.stack([_kmajor(np.asarray(inp["w_o"][l], f32), 1024) for l in range(L)]).reshape(L, 128, 4, 2048).transpose(0, 2, 1, 3).copy()
    wu = np.asarray(inp["ffn_w_up"], f32)
    wup = np.empty((L, NFC, 128, 2048), f32)
    for l in range(L):
        for c in range(NFC):
            cols = np.concatenate([c * 128 + np.arange(128), DFF + c * 128 + np.arange(128)])
            wup[l, c] = _kmajor(wu[l][:, cols], 256)
    out["w_up"] = wup
    fcw = np.asarray(inp["ffn_conv_w"], f32)
    out["ffn_cw"] = np.ascontiguousarray(fcw.reshape(L, 3, 44, 128).transpose(0, 3, 2, 1).reshape(L, 128, 132))
    out["ffn_cb"] = np.ascontiguousarray(np.asarray(inp["ffn_conv_b"], f32).reshape(L, 44, 128).transpose(0, 2, 1))
    out["w_down"] = np.stack([_kmajor(np.asarray(inp["ffn_w_down"][l], f32), 1024) for l in range(L)]).reshape(L, 128, 11, 2048).transpose(0, 2, 1, 3).copy()
    inv = (1.0 / (np.float32(10000.0) ** (np.arange(0, 64, 2, dtype=f32) / np.float32(64)))).astype(f32)
    ang = (np.arange(SEQ, dtype=f32)[:, None] * inv[None, :]).astype(f32)
    cos, sin = np.cos(ang).astype(f32).T, np.sin(ang).astype(f32).T
    out["rope_c"] = np.ascontiguousarray(np.concatenate([cos, cos, cos, cos], 0))
    out["rope_s"] = np.ascontiguousarray(np.concatenate([-sin, sin, -sin, sin], 0))
    k = np.arange(128)[:, None]
    q = np.arange(512)[None, :]
    msk = np.zeros((17, 128, 512), f32)
    for c in range(4):
        msk[c] = np.where(128 * c + k <= q, 0.0, NEGM)
        msk[4 + c] = np.where(128 * c + k > q, 0.0, NEGM)
    msk[8] = np.where(k > q, 0.0, NEGM)
    for c in range(4):
        msk[9 + c] = np.where((128 * c + k <= q) & (128 * c + k > q - 128), 0.0, NEGM)
    for i in range(4):
        msk[13 + i] = np.where(16 * k + 31 <= 512 * i + q, 0.0, NEGM)
    out["masks"] = _bf16(msk.transpose(1, 0, 2).reshape(128, 17 * 512))
    out["ident"] = _bf16(np.eye(128))
    E = np.zeros((128, 16, 128), f32)
    for kb in range(16):
        for kk in range(128):
            E[2 * kb + kk // 64, kb, kk] = 1.0
    out["sel_e"] = _bf16(E.reshape(128, 2048))
    n = np.arange(128)[:, None]
    j = np.arange(32)[None, :]
    ov = ((n * 16 < (j + 1) * 64) & (n * 16 + 32 > j * 64) & (n < 127)).astype(f32)
    out["overlap"] = _bf16(ov)
    FB = np.zeros((128, 8, 32), f32)
    VM = np.zeros((128, 8, 32), f32)
    for ii in range(2):
        for qs in range(4):
            t = 512 * (ii + 2) + 128 * qs + np.arange(128)
            cur = (t // 64)[:, None]
            jj = np.arange(32)[None, :]
            VM[:, ii * 4 + qs] = (jj <= cur)
            FB[:, ii * 4 + qs] = 1000.0 * ((jj == 0) | (jj == cur) | (jj == cur - 1))
    out["sel_tab"] = np.ascontiguousarray(np.concatenate([FB.reshape(128, 256), VM.reshape(128, 256), (VM - 1.0).reshape(128, 256)], axis=1))
    out["inv16"] = np.ascontiguousarray(np.broadcast_to((1.0 / np.arange(1, 17, dtype=f32))[None, :], (128, 16)))
    return out


class _Stop(Exception):
    pass


def build_program(nseq, nl, wshapes, taps=(), stop_after=None):
    nc = bass.Bass("TRN2", target_bir_lowering=False)
    D = {}
    for nm, (shp, dt) in wshapes.items():
        D[nm] = nc.dram_tensor(nm, list(shp), dt, kind="ExternalInput").ap()
    x_in = nc.dram_tensor("x", [nseq, SEQ, DM], F32, kind="ExternalInput").ap()
    y = nc.dram_tensor("y", [nseq, SEQ, DM], F32, kind="ExternalOutput").ap()
    scr_o = nc.dram_tensor("scr_o", [4, 512, SEQ], BF16).ap()
    tapd = {}
    TAPSHAPES = {"hT": [128, 8, SEQ], "od": [512, SEQ], "ob": [512, SEQ], "oc": [512, SEQ], "oa": [512, SEQ],
                 "x1": [SEQ, DM]}
    for nm in taps:
        tapd[nm] = nc.dram_tensor("tap_" + nm, TAPSHAPES[nm], BF16 if nm in ("hT", "od", "ob", "oc", "oa") else F32,
                                  kind="ExternalOutput").ap()

    with ExitStack() as es:
        S = Sched(nc, es)
        V, A, P, PE = nc.vector, nc.scalar, nc.gpsimd, nc.tensor

        uid = [0]

        def sb(st, name, shape, dt=F32):
            uid[0] += 1
            nm = "%s_u%d" % (name, uid[0])
            return B(st.enter_context(nc.sbuf_tensor(nm, shape, dt)), nm)

        stg = [sb(es, "stg%d" % i, [128, 2048], F32) for i in range(3)]
        wbf = [sb(es, "wbf%d" % i, [128, 2048], BF16) for i in range(3)]
        ident = sb(es, "ident", [128, 128], BF16)
        ones32 = sb(es, "ones32", [128, 128], F32)
        PS = [B(es.enter_context(nc.psum_tensor("ps%d" % i, [128, 512], F32)), "ps%d" % i, True) for i in range(6)]
        PB = [B(es.enter_context(nc.psum_tensor("pb%d" % i, [128, 1024], BF16)), "pb%d" % i, True) for i in range(2)]
        Ty = [[T("y%d_%d" % (s, tt)) for tt in range(NTT)] for s in range(nseq)]
        Tscr = [[T("scr%d_%d" % (n, i)) for i in range(NQT)] for n in range(4)]
        st_i = [0]
        ps_i = [0]
        pb_i = [0]
        fin = []

        S.dma(ident.t[:], D["ident"][:, :], writes=[ident.T])
        S.op("pool", lambda: P.memset(ones32.t[:], 1.0), [], [ones32.T])

        def tapdma(o, i_, reads=()):
            fin.append(S.dma(o, i_, reads=reads))

        stopped = [False]

        def phase_end(name):
            if stop_after == name:
                stopped[0] = True
            return stopped[0]

        def next_ps(lo=0, hi=4):
            i = lo + ps_i[0] % (hi - lo)
            ps_i[0] += 1
            return PS[i]

        def next_pb():
            pb_i[0] += 1
            return PB[pb_i[0] % 2]

        def load_cast(dst_ap, dstT, src_ap, ncols, eng="pool"):
            i = st_i[0] % 3
            st_i[0] += 1
            sg = stg[i]
            S.dma(sg.t[:, 0:ncols], src_ap, writes=[sg.T])
            if eng == "pool":
                S.op("pool", lambda: P.tensor_copy(dst_ap, sg.t[:, 0:ncols]), [sg.T], [dstT])
            elif eng == "dve":
                S.op("dve", lambda: V.tensor_copy(dst_ap, sg.t[:, 0:ncols]), [sg.T], [dstT])
            else:
                S.op("act", lambda: A.activation(out=dst_ap, in_=sg.t[:, 0:ncols], func=AF.Copy), [sg.T], [dstT])

        wb_i = [0]

        def load_wblock(src_ap, ncols, eng="pool"):
            wb = wbf[wb_i[0] % 3]
            wb_i[0] += 1
            load_cast(wb.t[:, 0:ncols], wb.T, src_ap, ncols, eng)
            return wb

        def mm(ps, out_ap, lhsT, rhs, start, stop, reads, skip=False):
            S.op("pe", lambda: PE.matmul(out_ap, lhsT, rhs, start=start, stop=stop, skip_group_check=skip),
                 reads, [ps.T])

        def rms_rstd(st, ss_ap, ssT, n_inv, name):
            l1 = sb(st, name + "_l1", [128, 1])
            r = sb(st, name + "_r", [128, 1])
            return l1, r

        def norm_transpose(st, s, l, src_fn, srcT_fn, gname, hT, hTT, col0):
            gb = sb(st, "gb", [128, DM])
            S.dma(gb.t[:], D[gname][l:l + 1, :].partition_broadcast(128), writes=[gb.T])
            xin = [sb(st, "xin%d" % i, [128, DM]) for i in range(2)]
            junk = sb(st, "junk", [128, DM], BF16)
            hn = [sb(st, "hn%d" % i, [128, DM], BF16) for i in range(2)]
            ss = sb(st, "ss", [128, 1])
            l1 = sb(st, "l1", [128, 1])
            rs = sb(st, "rs", [128, 1])
            for tt in range(NTT):
                xt = xin[tt % 2]
                S.dma(xt.t[:], src_fn(tt), reads=srcT_fn(tt), writes=[xt.T])
                S.op("act", lambda: A.activation(out=junk.t[:], in_=xt.t[:], func=AF.Square, accum_out=ss.t[:, 0:1]),
                     [xt.T], [junk.T, ss.T])
                S.op("act", lambda: A.activation(out=l1.t[:], in_=ss.t[:], func=AF.Ln, scale=1.0 / DM, bias=EPS),
                     [ss.T], [l1.T])
                S.op("act", lambda: A.activation(out=rs.t[:], in_=l1.t[:], func=AF.Exp, scale=-0.5), [l1.T], [rs.T])
                h = hn[tt % 2]
                S.op("dve", lambda: V.scalar_tensor_tensor(h.t[:], xt.t[:], rs.t[:, 0:1], gb.t[:], ALU.mult, ALU.mult),
                     [xt.T, rs.T, gb.T], [h.T])
                pb = next_pb()
                for kc in range(8):
                    S.op("pe", lambda kc=kc: PE.transpose(pb.t[:, kc * 128:(kc + 1) * 128], h.t[:, kc * 128:(kc + 1) * 128], ident.t[:]),
                         [h.T, ident.T], [pb.T])
                c0 = col0 + tt * 128
                S.op("act", lambda: A.activation(out=hT.t[:, :, c0:c0 + 128],
                                                 in_=pb.t[:, :].rearrange("p (k t) -> p k t", k=8), func=AF.Copy),
                     [pb.T], [hTT[tt // 4]])

        def out_epilogue(st, s, l, gname, get_ps, nm):
            gb = sb(st, nm + "gb", [128, DM])
            S.dma(gb.t[:], D[gname][l:l + 1, :].partition_broadcast(128), writes=[gb.T])
            xin = [sb(st, nm + "xin%d" % i, [128, DM]) for i in range(2)]
            tmp = [sb(st, nm + "tmp%d" % i, [128, DM]) for i in range(2)]
            junk = sb(st, nm + "junk", [128, 512], BF16)
            ss = sb(st, nm + "ss", [128, 2])
            s1 = sb(st, nm + "s1", [128, 1])
            l1 = sb(st, nm + "l1", [128, 1])
            rs = sb(st, nm + "rs", [128, 1])
            for tt in range(NTT):
                xt = xin[tt % 2]
                tm = tmp[tt % 2]
                src = x_in if (l == 0 and nm == "mo") else y
                S.dma(xt.t[:], src[s, tt * 128:(tt + 1) * 128, :], reads=[Ty[s][tt]], writes=[xt.T])
                pa, pbk = get_ps(tt)
                for hf, pp in enumerate((pa, pbk)):
                    S.op("act", lambda hf=hf, pp=pp: A.activation(out=junk.t[:], in_=pp.t[:], func=AF.Square,
                                                                  accum_out=ss.t[:, hf:hf + 1]),
                         [pp.T], [junk.T, ss.T])
                S.op("dve", lambda: V.tensor_tensor(s1.t[:], ss.t[:, 0:1], ss.t[:, 1:2], ALU.add), [ss.T], [s1.T])
                S.op("act", lambda: A.activation(out=l1.t[:], in_=s1.t[:], func=AF.Ln, scale=1.0 / DM, bias=EPS),
                     [s1.T], [l1.T])
                S.op("act", lambda: A.activation(out=rs.t[:], in_=l1.t[:], func=AF.Exp, scale=-0.5), [l1.T], [rs.T])
                for hf, pp in enumerate((pa, pbk)):
                    S.op("dve", lambda hf=hf, pp=pp: V.scalar_tensor_tensor(
                        tm.t[:, hf * 512:(hf + 1) * 512], pp.t[:], rs.t[:, 0:1], gb.t[:, hf * 512:(hf + 1) * 512],
                        ALU.mult, ALU.mult), [pp.T, rs.T, gb.T], [tm.T])
                S.op("pool", lambda: P.tensor_tensor(tm.t[:], tm.t[:], xt.t[:], ALU.add), [tm.T, xt.T], [tm.T])
                tok = S.dma(y[s, tt * 128:(tt + 1) * 128, :], tm.t[:], reads=[tm.T], writes=[Ty[s][tt]])
                if nm == "fo" and l == nl - 1:
                    fin.append(tok)

        def mixer(s, l):
            with ExitStack() as mx:
                hT = sb(mx, "hT", [128, 8, SEQ], BF16)
                hTT = [T("hT%d" % i) for i in range(NQT)]
                av = mx.enter_context(ExitStack())
                vall = sb(av, "vall", [128, NTT, 6, 65], BF16)
                ga = sb(av, "ga", [128, NTT, 24])
                with ExitStack() as st:
                    src = x_in if l == 0 else y
                    norm_transpose(st, s, l, lambda tt: src[s, tt * 128:(tt + 1) * 128, :],
                                   lambda tt: [Ty[s][tt]], "norm_mix_pre", hT, hTT, 0)
                    S.barrier()
                if "hT" in tapd and s == 0 and l == 0:
                    tapdma(tapd["hT"][:, :, :], hT.t[:], reads=hTT)
                if phase_end("P0"):
                    return
                with ExitStack() as st:
                    wtm = sb(st, "wtm", [128, 8, NTM], BF16)
                    for hfp in range(2):
                        load_cast(wtm.t[:, hfp * 4:(hfp + 1) * 4, :].rearrange("p a b -> p (a b)"), wtm.T,
                                  D["win_tm"][l, :, hfp * 4 * NTM:(hfp + 1) * 4 * NTM], 4 * NTM)
                    S.op("pool", lambda: P.memset(vall.t[:, :, :, 64:65], 1.0), [], [vall.T])
                    for tt in range(NTT):
                        ps = next_ps()
                        for kc in range(8):
                            mm(ps, ps.t[:, 0:NTM], hT.t[:, kc, tt * 128:(tt + 1) * 128], wtm.t[:, kc, :], kc == 0, kc == 7,
                               [hTT[tt // 4], wtm.T])
                        S.op("act", lambda: A.activation(out=vall.t[:, tt, :, 0:64],
                                                         in_=ps.t[:, 0:384].rearrange("p (a b) -> p a b", a=6), func=AF.Copy),
                             [ps.T], [vall.T])
                        S.op("act", lambda: A.activation(out=ga.t[:, tt, :], in_=ps.t[:, 384:408], func=AF.Sigmoid),
                             [ps.T], [ga.T])
                    S.barrier()

                def proj_pair(wblk, i, reads_extra=()):
                    pa = next_ps()
                    pbk = next_ps()
                    for half, pp in enumerate((pa, pbk)):
                        for kc in range(8):
                            mm(pp, pp.t[:, :], wblk.t[:, kc * 256 + half * 128: kc * 256 + half * 128 + 128],
                               hT.t[:, kc, i * TQ:(i + 1) * TQ], kc == 0, kc == 7, [hTT[i], wblk.T])
                    return pa, pbk

                def rope_evac(st_bufs, pa, pbk, i, dst_ap, dstT, rc, rsn):
                    t1, t2 = st_bufs
                    S.op("dve", lambda: V.tensor_tensor(t1.t[:], pa.t[:], rc.t[:, i * TQ:(i + 1) * TQ], ALU.mult),
                         [pa.T, rc.T], [t1.T])
                    S.op("dve", lambda: V.tensor_tensor(t2.t[:], pbk.t[:], rsn.t[:, i * TQ:(i + 1) * TQ], ALU.mult),
                         [pbk.T, rsn.T], [t2.T])
                    S.op("pool", lambda: P.tensor_tensor(dst_ap, t1.t[:], t2.t[:], ALU.add), [t1.T, t2.T], [dstT])

                def run_blocks(blocks, tag=""):
                    pend = None
                    if tag == "cmp":
                        for blk in blocks:
                            pt = blk[0]()
                            blk[1](pt)
                            if blk[2] is not None:
                                blk[2]()
                        return
                    for blk in blocks:
                        pt = blk[0]()
                        if pend is not None:
                            pend[0][1](pend[1])
                            if pend[0][2] is not None:
                                pend[0][2]()
                        pend = (blk, pt)
                    if pend is not None:
                        pend[0][1](pend[1])
                        if pend[0][2] is not None:
                            pend[0][2]()

                def rope_evac_k(st_bufs, pa, pbk, i, kA, kB, rc, rsn):
                    t1, t2 = st_bufs
                    sl = slice(i * TQ, (i + 1) * TQ)
                    S.op("dve", lambda: V.tensor_tensor(t1.t[:], pa.t[:], rc.t[:, sl], ALU.mult), [pa.T, rc.T], [t1.T])
                    S.op("dve", lambda: V.tensor_tensor(t2.t[:], pbk.t[:], rsn.t[:, sl], ALU.mult), [pbk.T, rsn.T], [t2.T])
                    S.op("pool", lambda: P.tensor_tensor(kA.t[0:64, sl], t1.t[0:64, :], t2.t[0:64, :], ALU.add), [t1.T, t2.T], [kA.T])
                    S.op("pool", lambda: P.tensor_tensor(kB.t[64:128, sl], t1.t[64:128, :], t2.t[64:128, :], ALU.add), [t1.T, t2.T], [kB.T])

                def transpose_store(st_bufs, tok_tile, tokT, n, i, tapname):
                    oT = st_bufs
                    for qs in range(4):
                        pb = next_pb()
                        for cc in range(4):
                            S.op("pe", lambda cc=cc: PE.transpose(pb.t[:, cc * 128:(cc + 1) * 128],
                                                                  tok_tile.t[:, qs, cc * 128:(cc + 1) * 128], ident.t[:]),
                                 [tokT, ident.T], [pb.T])
                        S.op("dve", lambda: V.tensor_copy(oT.t[:, :, qs * 128:(qs + 1) * 128],
                                                          pb.t[:, 0:512].rearrange("p (c t) -> p c t", c=4)),
                             [pb.T], [oT.T])
                    for cc in range(4):
                        S.dma(scr_o[n, cc * 128:(cc + 1) * 128, i * TQ:(i + 1) * TQ], oT.t[:, cc, :], reads=[oT.T],
                              writes=[Tscr[n][i]])
                        if tapname in tapd and s == 0 and l == 0:
                            tapdma(tapd[tapname][cc * 128:(cc + 1) * 128, i * TQ:(i + 1) * TQ], oT.t[:, cc, :], reads=[oT.T])

                if phase_end("P0b"):
                    return
                with ExitStack() as st:
                    rc = sb(st, "rc", [128, SEQ])
                    rsn = sb(st, "rsn", [128, SEQ])
                    S.dma(rc.t[:], D["rope_c"][:, :], writes=[rc.T])
                    S.dma(rsn.t[:], D["rope_s"][:, :], writes=[rsn.T])
                    msk = sb(st, "mskd", [128, 5, 512], BF16)
                    S.dma(msk.t[:].rearrange("p a b -> p (a b)"), D["masks"][:, 8 * 512:13 * 512], writes=[msk.T])
                    esk = sb(st, "esk", [128, 8])
                    S.dma(esk.t[:], D["swa_sinks"][l:l + 1, :].partition_broadcast(128), writes=[esk.T])
                    S.op("act", lambda: A.activation(out=esk.t[:], in_=esk.t[:], func=AF.Exp), [esk.T], [esk.T])
                    qd = [sb(st, "qd%d" % i, [128, SEQ], BF16) for i in range(4)]
                    kd = [[sb(st, "kd%d_%d" % (i, v), [128, SEQ], BF16) for v in range(2)] for i in range(2)]
                    for g in range(2):
                        for v in range(2):
                            S.op("pool", lambda g=g, v=v: P.memset(kd[g][v].t[:], 0.0), [], [kd[g][v].T])
                    t12 = [(sb(st, "rt1_%d" % i, [128, 512]), sb(st, "rt2_%d" % i, [128, 512])) for i in range(2)]
                    ri = 0
                    for b in range(6):
                        wblk = load_wblock(D["win_fm"][l, b], 2048)
                        for i in range(NQT):
                            pa, pbk = proj_pair(wblk, i)
                            if b < 4:
                                rope_evac(t12[ri % 2], pa, pbk, i, qd[b].t[:, i * TQ:(i + 1) * TQ], qd[b].T, rc, rsn)
                            else:
                                rope_evac_k(t12[ri % 2], pa, pbk, i, kd[b - 4][0], kd[b - 4][1], rc, rsn)
                            ri += 1
                    pts = [sb(st, "ptd%d" % i, [128, 512], BF16) for i in range(3)]
                    pti = [0]
                    odt = [sb(st, "odt%d" % i, [128, 4, 512], BF16) for i in range(2)]
                    oTs = [sb(st, "oTd%d" % i, [128, 4, 512], BF16) for i in range(2)]
                    dn = sb(st, "dnd", [128, 4])
                    rcp = sb(st, "rcd", [128, 4])
                    for i in range(NQT):
                        od = odt[i % 2]
                        blocks = []
                        for h in range(8):
                            hp, hv, g = h // 2, h % 2, h // 4
                            pv = PS[4 + h % 2]
                            pvv = pv.t[:, 0:260].rearrange("p (a b) -> p a b", a=4)
                            plan = []
                            for c in range(-1, 4):
                                kb = 4 * i + c
                                if kb < 0:
                                    continue
                                lo, hi = max(0, 128 * c), min(512, 128 * c + 256)
                                plan.append((c, kb, lo, hi))
                            nmm = sum((hi - lo) // 128 for (_, _, lo, hi) in plan)
                            kcnt = [0]

                            def fin_head(h=h, pv=pv, pvv=pvv):
                                S.op("dve", lambda: V.tensor_scalar(dn.t[:], pvv[:, :, 64], esk.t[:, h:h + 1], None, ALU.add),
                                     [pv.T, esk.T], [dn.T])
                                S.op("dve", lambda: V.reciprocal(rcp.t[:], dn.t[:]), [dn.T], [rcp.T])
                                S.op("dve", lambda: V.tensor_tensor(
                                    od.t[:, :, h * 64:(h + 1) * 64], pvv[:, :, 0:64],
                                    rcp.t[:].rearrange("p (a o) -> p a o", o=1).to_broadcast([128, 4, 64]), ALU.mult),
                                    [pv.T, rcp.T], [od.T])

                            for bi, (c, kb, lo, hi) in enumerate(plan):
                                def score(c=c, kb=kb, lo=lo, hi=hi, hp=hp, hv=hv, g=g):
                                    n = hi - lo
                                    sc = next_ps()
                                    kk = kd[g][hv]
                                    mm(sc, sc.t[:, 0:n], kk.t[:, kb * 128:(kb + 1) * 128],
                                       qd[hp].t[:, i * TQ + lo:i * TQ + hi], True, False, [kk.T, qd[hp].T])
                                    mm(sc, sc.t[:, 0:n], ident.t[:], msk.t[:, c + 1, lo:hi], False, True, [ident.T, msk.T])
                                    pt = pts[pti[0] % 3]
                                    pti[0] += 1
                                    S.op("act", lambda: A.activation(out=pt.t[:, 0:n], in_=sc.t[:, 0:n], func=AF.Exp, scale=0.125),
                                         [sc.T], [pt.T])
                                    return pt

                                def pvf(pt, kb=kb, lo=lo, hi=hi, g=g, pv=pv, pvv=pvv, kcnt=kcnt, nmm=nmm):
                                    for qs in range(lo // 128, hi // 128):
                                        mm(pv, pvv[:, qs, :], pt.t[:, qs * 128 - lo:qs * 128 - lo + 128],
                                           vall.t[:, kb, 4 + g, :], kcnt[0] == 0, kcnt[0] == nmm - 1, [pt.T, vall.T], skip=True)
                                        kcnt[0] += 1

                                blocks.append((score, pvf, fin_head if bi == len(plan) - 1 else None))
                        run_blocks(blocks)
                        transpose_store(oTs[i % 2], od, od.T, 3, i, "od")
                    S.barrier()

                if phase_end("P1"):
                    return
                with ExitStack() as st:
                    cw = sb(st, "cw", [128, 4, 31])
                    S.dma(cw.t[:].rearrange("p a b -> p (a b)"), D["conv_w"][l], writes=[cw.T])
                    cprm = sb(st, "cprm", [128, 3, 4])
                    for j, nm in enumerate(("conv_b", "conv_ln_g", "conv_ln_b")):
                        S.dma(cprm.t[:, j, :], D[nm][l], writes=[cprm.T])
                    cbuf = sb(st, "cbuf", [128, 4, 30 + SEQ])
                    S.op("pool", lambda: P.memset(cbuf.t[:, :, 0:30], 0.0), [], [cbuf.T])
                    sg = [sb(st, "sg%d" % i, [128, 512]) for i in range(2)]
                    k = 0
                    for cc in range(4):
                        wblk = load_wblock(D["win_fm"][l, 6 + cc], 2048)
                        for i in range(NQT):
                            pa, pbk = proj_pair(wblk, i)
                            sgi = sg[k % 2]
                            k += 1
                            S.op("act", lambda: A.activation(out=sgi.t[:], in_=pbk.t[:], func=AF.Sigmoid), [pbk.T], [sgi.T])
                            S.op("dve", lambda: V.tensor_tensor(cbuf.t[:, cc, 30 + i * TQ:30 + (i + 1) * TQ], pa.t[:], sgi.t[:], ALU.mult),
                                 [pa.T, sgi.T], [cbuf.T])
                    conv = [sb(st, "conv%d" % i, [128, SEQ]) for i in range(4)]
                    accb = sb(st, "accb", [128, SEQ])
                    tmpb = sb(st, "tmpb", [128, SEQ])
                    for cc in range(4):
                        ca = conv[cc]
                        S.op("act", lambda: A.activation(out=ca.t[:], in_=cbuf.t[:, cc, 30:30 + SEQ], func=AF.Identity,
                                                         scale=cw.t[:, cc, 30:31], bias=cprm.t[:, 0, cc:cc + 1]),
                             [cbuf.T, cw.T, cprm.T], [ca.T])
                        for j in range(0, 20):
                            S.op("dve", lambda j=j: V.scalar_tensor_tensor(ca.t[:], cbuf.t[:, cc, j:j + SEQ], cw.t[:, cc, j:j + 1],
                                                                           ca.t[:], ALU.mult, ALU.add), [cbuf.T, cw.T, ca.T], [ca.T])
                        S.op("pool", lambda: P.tensor_scalar(accb.t[:], cbuf.t[:, cc, 20:20 + SEQ], cw.t[:, cc, 20:21], 0.0,
                                                             ALU.mult, ALU.add), [cbuf.T, cw.T], [accb.T])
                        for j in range(21, 30):
                            S.op("pool", lambda j=j: P.tensor_scalar(tmpb.t[:], cbuf.t[:, cc, j:j + SEQ], cw.t[:, cc, j:j + 1], 0.0,
                                                                     ALU.mult, ALU.add), [cbuf.T, cw.T], [tmpb.T])
                            S.op("pool", lambda: P.tensor_tensor(accb.t[:], accb.t[:], tmpb.t[:], ALU.add), [accb.T, tmpb.T], [accb.T])
                        S.op("pool", lambda: P.tensor_tensor(ca.t[:], ca.t[:], accb.t[:], ALU.add), [ca.T, accb.T], [ca.T])
                    sq = sb(st, "sq", [128, 4, 512])
                    mean = sb(st, "mean", [128, 512])
                    msq = sb(st, "msq", [128, 512])
                    var = sb(st, "var", [128, 512])
                    rstd = sb(st, "rstd", [128, 512])
                    dd = [sb(st, "dd%d" % i, [128, 512]) for i in range(2)]
                    obT = [sb(st, "obT%d" % i, [128, 4, 512], BF16) for i in range(2)]
                    for i in range(NQT):
                        sl = slice(i * TQ, (i + 1) * TQ)
                        S.op("act", lambda: A.activation(out=sq.t[:, 0, :], in_=conv[0].t[:, sl], func=AF.Square), [conv[0].T], [sq.T])
                        S.op("act", lambda: A.activation(out=sq.t[:, 1, :], in_=conv[1].t[:, sl], func=AF.Square), [conv[1].T], [sq.T])
                        S.op("act", lambda: A.activation(out=sq.t[:, 2, :], in_=conv[2].t[:, sl], func=AF.Square), [conv[2].T], [sq.T])
                        S.op("act", lambda: A.activation(out=sq.t[:, 3, :], in_=conv[3].t[:, sl], func=AF.Square), [conv[3].T], [sq.T])
                        p1 = next_ps()
                        p2 = next_ps()
                        for cc in range(4):
                            mm(p1, p1.t[:, :], ones32.t[:], conv[cc].t[:, sl], cc == 0, cc == 3, [ones32.T, conv[cc].T])
                        for cc in range(4):
                            mm(p2, p2.t[:, :], ones32.t[:], sq.t[:, cc, :], cc == 0, cc == 3, [ones32.T, sq.T])
                        S.op("dve", lambda: V.tensor_scalar(mean.t[:], p1.t[:], 1.0 / 512, None, ALU.mult), [p1.T], [mean.T])
                        S.op("dve", lambda: V.tensor_tensor(msq.t[:], mean.t[:], mean.t[:], ALU.mult), [mean.T], [msq.T])
                        S.op("dve", lambda: V.scalar_tensor_tensor(var.t[:], p2.t[:], 1.0 / 512, msq.t[:], ALU.mult, ALU.subtract),
                             [p2.T, msq.T], [var.T])
                        S.op("act", lambda: A.activation(out=var.t[:], in_=var.t[:], func=AF.Ln, bias=EPS), [var.T], [var.T])
                        S.op("act", lambda: A.activation(out=rstd.t[:], in_=var.t[:], func=AF.Exp, scale=-0.5), [var.T], [rstd.T])
                        ob = obT[i % 2]
                        for cc in range(4):
                            d = dd[cc % 2]
                            S.op("dve", lambda: V.tensor_tensor(d.t[:], conv[cc].t[:, sl], mean.t[:], ALU.subtract), [conv[cc].T, mean.T], [d.T])
                            S.op("dve", lambda: V.tensor_tensor(d.t[:], d.t[:], rstd.t[:], ALU.mult), [d.T, rstd.T], [d.T])
                            S.op("act", lambda: A.activation(out=ob.t[:, cc, :], in_=d.t[:], func=AF.Silu,
                                                             scale=cprm.t[:, 1, cc:cc + 1], bias=cprm.t[:, 2, cc:cc + 1]),
                                 [d.T, cprm.T], [ob.T])
                        for cc in range(4):
                            S.dma(scr_o[1, cc * 128:(cc + 1) * 128, sl], ob.t[:, cc, :], reads=[ob.T], writes=[Tscr[1][i]])
                            if "ob" in tapd and s == 0 and l == 0:
                                tapdma(tapd["ob"][cc * 128:(cc + 1) * 128, sl], ob.t[:, cc, :], reads=[ob.T])
                    S.barrier()

                if phase_end("P2"):
                    return
                with ExitStack() as st:
                    PADC = 16
                    ub = [sb(st, "ub%d" % i, [128, PADC + SEQ]) for i in range(4)]
                    pp = [sb(st, "pp%d" % i, [128, PADC + SEQ]) for i in range(2)]
                    for bfr in ub + pp:
                        S.op("pool", lambda bfr=bfr: P.memset(bfr.t[:, 0:PADC], 0.0), [], [bfr.T])
                    inv16 = sb(st, "inv16", [128, 16])
                    S.dma(inv16.t[:], D["inv16"][:, :], writes=[inv16.T])
                    pw = sb(st, "pw", [128, 4, 128], BF16)
                    load_cast(pw.t[:].rearrange("p a b -> p (a b)"), pw.T, D["pool_w"][l], 512)
                    psc = sb(st, "psc", [128, 4])
                    S.dma(psc.t[:], D["pool_scale"][l], writes=[psc.T])
                    for j in range(2):
                        wblk = load_wblock(D["win_fm"][l, 10 + j], 2048)
                        for i in range(NQT):
                            pa, pbk = proj_pair(wblk, i)
                            S.op("act", lambda: A.activation(out=ub[2 * j].t[:, PADC + i * TQ:PADC + (i + 1) * TQ], in_=pa.t[:], func=AF.Copy),
                                 [pa.T], [ub[2 * j].T])
                            S.op("dve", lambda: V.tensor_copy(ub[2 * j + 1].t[:, PADC + i * TQ:PADC + (i + 1) * TQ], pbk.t[:]),
                                 [pbk.T], [ub[2 * j + 1].T])
                    pooled = [sb(st, "pooled%d" % i, [128, SEQ], BF16) for i in range(2)]
                    tmpf = sb(st, "ptmp", [128, 16])
                    ocT = [sb(st, "ocT%d" % i, [128, SEQ], BF16) for i in range(2)]
                    for g in range(4):
                        W = 2 << g
                        src = ub[g]
                        for lev in range(g + 1):
                            sh = 1 << lev
                            dst = pp[lev % 2]
                            S.op("pool", lambda src=src, dst=dst, sh=sh: P.tensor_tensor(
                                dst.t[:, PADC:PADC + SEQ], src.t[:, PADC:PADC + SEQ], src.t[:, PADC - sh:PADC + SEQ - sh], ALU.add),
                                [src.T], [dst.T])
                            src = dst
                        pl = pooled[g % 2]
                        u = ub[g]
                        S.op("dve", lambda: V.scalar_tensor_tensor(pl.t[:], src.t[:, PADC:PADC + SEQ], 1.0 / W, u.t[:, PADC:PADC + SEQ],
                                                                   ALU.mult, ALU.subtract), [src.T, u.T], [pl.T])
                        S.op("dve", lambda: V.tensor_tensor(tmpf.t[:, 0:W - 1], src.t[:, PADC:PADC + W - 1], inv16.t[:, 0:W - 1], ALU.mult),
                             [src.T, inv16.T], [tmpf.T])
                        S.op("dve", lambda: V.tensor_tensor(pl.t[:, 0:W - 1], tmpf.t[:, 0:W - 1], u.t[:, PADC:PADC + W - 1], ALU.subtract),
                             [tmpf.T, u.T], [pl.T])
                        oc = ocT[g % 2]
                        for i in range(NQT):
                            ps = next_ps()
                            mm(ps, ps.t[:, :], pw.t[:, g, :], pl.t[:, i * TQ:(i + 1) * TQ], True, True, [pw.T, pl.T])
                            S.op("act", lambda: A.activation(out=oc.t[:, i * TQ:(i + 1) * TQ], in_=ps.t[:], func=AF.Copy,
                                                             scale=psc.t[:, g:g + 1]), [ps.T, psc.T], [oc.T])
                        S.dma(scr_o[2, g * 128:(g + 1) * 128, :], oc.t[:], reads=[oc.T], writes=Tscr[2])
                        if "oc" in tapd and s == 0 and l == 0:
                            tapdma(tapd["oc"][g * 128:(g + 1) * 128, :], oc.t[:], reads=[oc.T])
                    S.barrier()

                if phase_end("P3"):
                    return
                with ExitStack() as st:
                    msk = sb(st, "mska", [128, 8, 512], BF16)
                    S.dma(msk.t[:].rearrange("p a b -> p (a b)"), D["masks"][:, 0:8 * 512], writes=[msk.T])
                    mcmp = sb(st, "mcmp", [128, 4, 512], BF16)
                    S.dma(mcmp.t[:].rearrange("p a b -> p (a b)"), D["masks"][:, 13 * 512:17 * 512], writes=[mcmp.T])
                    selE = sb(st, "selE", [128, 16, 128], BF16)
                    S.dma(selE.t[:].rearrange("p a b -> p (a b)"), D["sel_e"][:, :], writes=[selE.T])
                    stab = sb(st, "stab", [128, 3, 8, 32])
                    S.dma(stab.t[:].rearrange("p a b c -> p (a b c)"), D["sel_tab"][:, :], writes=[stab.T])
                    qraw = [sb(st, "qraw%d" % i, [128, SEQ], BF16) for i in range(4)]
                    qrot = [sb(st, "qrot%d" % i, [128, SEQ], BF16) for i in range(4)]
                    ksl = [[sb(st, "ksl%d_%d" % (i, v), [128, SEQ], BF16) for v in range(2)] for i in range(2)]
                    kwn = [[sb(st, "kwn%d_%d" % (i, v), [128, SEQ], BF16) for v in range(2)] for i in range(2)]
                    for kk_ in ksl + kwn:
                        for v in range(2):
                            S.op("pool", lambda kk_=kk_, v=v: P.memset(kk_[v].t[:], 0.0), [], [kk_[v].T])
                    kcT = sb(st, "kcT", [128, 2, 2, 128], BF16)
                    vcx = sb(st, "vcx", [128, 2, 98], BF16)
                    ovl = sb(st, "ovl", [128, 32], BF16)
                    stc = st.enter_context(ExitStack())
                    csrc = [sb(stc, "csrc%d" % i, [128, SEQ], BF16) for i in range(2)]
                    strp = st.enter_context(ExitStack())
                    rc = sb(strp, "rc", [128, SEQ])
                    rsn = sb(strp, "rsn", [128, SEQ])
                    S.dma(rc.t[:], D["rope_c"][:, :], writes=[rc.T])
                    S.dma(rsn.t[:], D["rope_s"][:, :], writes=[rsn.T])
                    t12 = [(sb(strp, "rt1_%d" % i, [128, 512]), sb(strp, "rt2_%d" % i, [128, 512])) for i in range(2)]
                    ri = 0
                    if phase_end("P4s"):
                        return
                    for b in range(9):
                        wblk = load_wblock(D["win_fm"][l, 12 + b], 2048)
                        for i in range(NQT):
                            pa, pbk = proj_pair(wblk, i)
                            sl = slice(i * TQ, (i + 1) * TQ)
                            if b < 4:
                                S.op("act", lambda: A.activation(out=qraw[b].t[:, sl], in_=pa.t[:], func=AF.Copy), [pa.T], [qraw[b].T])
                                rope_evac(t12[ri % 2], pa, pbk, i, qrot[b].t[:, sl], qrot[b].T, rc, rsn)
                                ri += 1
                            elif b < 8:
                                kk_ = ksl[b - 4] if b < 6 else kwn[b - 6]
                                rope_evac_k(t12[ri % 2], pa, pbk, i, kk_[0], kk_[1], rc, rsn)
                                ri += 1
                            else:
                                S.op("act", lambda: A.activation(out=csrc[0].t[:, sl], in_=pa.t[:], func=AF.Copy), [pa.T], [csrc[0].T])
                                S.op("dve", lambda: V.tensor_copy(csrc[1].t[:, sl], pbk.t[:]), [pbk.T], [csrc[1].T])
                    S.barrier()
                    strp.close()
                    if phase_end("P4a"):
                        return
                    S.dma(ovl.t[:], D["overlap"][:, :], writes=[ovl.T])
                    S.op("pool", lambda: P.memset(vcx.t[:], 0.0), [], [vcx.T])
                    S.op("pool", lambda: P.memset(kcT.t[:], 0.0), [], [kcT.T])
                    S.op("pool", lambda: P.memset(vcx.t[:, :, 64:65], 1.0), [], [vcx.T])
                    for g in range(2):
                        S.op("dve", lambda g=g: V.tensor_copy(vcx.t[:, g, 66:98], ovl.t[:]), [ovl.T], [vcx.T])
                    with ExitStack() as st2:
                        w1 = sb(st2, "w1", [128, 32, 256], BF16)
                        kblk = sb(st2, "kblk", [128, 32, 128], BF16)
                        posb = sb(st2, "posb", [128, 32])
                        hid = sb(st2, "hid", [128, 2, 2, 128], BF16)
                        w2k = sb(st2, "w2k", [128, 2, 128], BF16)
                        w2v = sb(st2, "w2v", [128, 2, 64], BF16)
                        load_cast(w2k.t[:].rearrange("p a b -> p (a b)"), w2k.T, D["cmp_w2k"][l], 256)
                        load_cast(w2v.t[:].rearrange("p a b -> p (a b)"), w2v.T, D["cmp_w2v"][l], 128)
                        for kv in range(2):
                            for pc in range(4):
                                load_cast(w1.t[:, pc * 8:(pc + 1) * 8, :].rearrange("p a b -> p (a b)"), w1.T,
                                          D["cmp_w1"][l, kv, pc], 2048)
                            S.dma(posb.t[:], D["cmp_pos"][l, kv], writes=[posb.T])
                            srcv = csrc[kv].t[:, 0:SEQ].rearrange("p (n s) -> p s n", s=16)
                            for hl in range(2):
                                S.op("dve", lambda hl=hl: V.tensor_tensor(
                                    kblk.t[:, hl * 16:(hl + 1) * 16, 0:127], srcv[:, :, hl:hl + 127],
                                    posb.t[:, hl * 16:(hl + 1) * 16].rearrange("p (a o) -> p a o", o=1).to_broadcast([128, 16, 127]),
                                    ALU.add), [csrc[kv].T, posb.T], [kblk.T])
                            for hc in range(2):
                                for g in range(2):
                                    ps = next_ps()
                                    for ll in range(32):
                                        mm(ps, ps.t[:, 0:127], w1.t[g * 64:(g + 1) * 64, ll, hc * 128:(hc + 1) * 128],
                                           kblk.t[g * 64:(g + 1) * 64, ll, 0:127], ll == 0, ll == 31, [w1.T, kblk.T])
                                    S.op("act", lambda g=g: A.activation(out=hid.t[:, hc, g, 0:127], in_=ps.t[:, 0:127],
                                                                         func=AF.Gelu_apprx_tanh), [ps.T], [hid.T])
                            if kv == 0:
                                ps = next_ps()
                                for g in range(2):
                                    for hc in range(2):
                                        mm(ps, ps.t[:, g * 128:g * 128 + 127], w2k.t[:, hc, :], hid.t[:, hc, g, 0:127],
                                           (g == 0 and hc == 0), (g == 1 and hc == 1), [w2k.T, hid.T], skip=True)
                                for g in range(2):
                                    S.op("dve", lambda g=g: V.tensor_copy(kcT.t[0:64, g, 0, 0:127], ps.t[0:64, g * 128:g * 128 + 127]), [ps.T], [kcT.T])
                                    S.op("dve", lambda g=g: V.tensor_copy(kcT.t[64:128, g, 1, 0:127], ps.t[64:128, g * 128:g * 128 + 127]), [ps.T], [kcT.T])
                            else:
                                for g in range(2):
                                    ps = next_ps()
                                    for hc in range(2):
                                        mm(ps, ps.t[0:127, 0:64], hid.t[:, hc, g, 0:127], w2v.t[:, hc, :], hc == 0, hc == 1,
                                           [w2v.T, hid.T])
                                    S.op("dve", lambda g=g: V.tensor_copy(vcx.t[0:127, g, 0:64], ps.t[0:127, 0:64]), [ps.T], [vcx.T])
                        S.barrier()
                    stc.close()
                    if phase_end("P4b"):
                        return
                    pts = [sb(st, "pta%d" % i, [128, 512], BF16) for i in range(3)]
                    pti = [0]
                    oacc = sb(st, "oacc", [128, 4, 512])
                    oabf = [sb(st, "oabf%d" % i, [128, 4, 512], BF16) for i in range(2)]
                    oTs = [sb(st, "oTa%d" % i, [128, 4, 512], BF16) for i in range(2)]
                    imp = sb(st, "imp", [128, 4, 2, 32])
                    dn = sb(st, "dna", [128, 4])
                    rcp = sb(st, "rca", [128, 4])
                    ff = sb(st, "ffa", [128, 4])
                    tmpo = sb(st, "tmpo", [128, 4, 64])
                    tmpi = sb(st, "tmpi", [128, 4, 32])
                    score_t = sb(st, "score", [128, 4, 32])
                    mx8 = sb(st, "mx8", [128, 16])
                    mrep = sb(st, "mrep", [128, 32])
                    sel = sb(st, "sel", [128, 32])
                    selb = sb(st, "selb", [128, 32], BF16)
                    selbT = [sb(st, "selbT%d" % i, [128, 512], BF16) for i in range(2)]
                    for g in range(2):
                        S.op("pool", lambda g=g: P.memset(selbT[g].t[:], 0.0), [], [selbT[g].T])

                    def finish_head(pv, pvv, i, h, br, first, guard):
                        if guard:
                            S.op("dve", lambda: V.tensor_scalar(dn.t[:], pvv[:, :, 64], 1e-30, None, ALU.max), [pv.T], [dn.T])
                            S.op("dve", lambda: V.reciprocal(rcp.t[:], dn.t[:]), [dn.T], [rcp.T])
                        else:
                            S.op("dve", lambda: V.reciprocal(rcp.t[:], pvv[:, :, 64]), [pv.T], [rcp.T])
                        S.op("dve", lambda: V.tensor_tensor(ff.t[:], rcp.t[:], ga.t[:, i * 4:(i + 1) * 4, br * 8 + h], ALU.mult),
                             [rcp.T, ga.T], [ff.T])
                        fb = ff.t[:].rearrange("p (a o) -> p a o", o=1).to_broadcast([128, 4, 64])
                        if first:
                            S.op("dve", lambda: V.tensor_tensor(oacc.t[:, :, h * 64:(h + 1) * 64], pvv[:, :, 0:64], fb, ALU.mult),
                                 [pv.T, ff.T], [oacc.T])
                        else:
                            S.op("dve", lambda: V.tensor_tensor(tmpo.t[:], pvv[:, :, 0:64], fb, ALU.mult), [pv.T, ff.T], [tmpo.T])
                            S.op("pool", lambda: P.tensor_tensor(oacc.t[:, :, h * 64:(h + 1) * 64], oacc.t[:, :, h * 64:(h + 1) * 64],
                                                                 tmpo.t[:], ALU.add), [oacc.T, tmpo.T], [oacc.T])

                    def make_score(n, k_ap, kT_, q_ap, qT_, extra, np_=128):
                        def score():
                            sc = next_ps()
                            mm(sc, sc.t[0:np_, 0:n], k_ap, q_ap, True, len(extra) == 0, [kT_, qT_])
                            for e_i, (l_ap, lT, r_ap, rT) in enumerate(extra):
                                mm(sc, sc.t[0:np_, 0:n], l_ap, r_ap, False, e_i == len(extra) - 1, [lT, rT])
                            pt = pts[pti[0] % 3]
                            pti[0] += 1
                            S.op("act", lambda: A.activation(out=pt.t[0:np_, 0:n], in_=sc.t[0:np_, 0:n], func=AF.Exp, scale=0.125), [sc.T], [pt.T])
                            return pt
                        return score

                    for i in range(NQT):
                        qsl = slice(i * TQ, (i + 1) * TQ)
                        blocks = []
                        for h in range(8):
                            hp, hv, g = h // 2, h % 2, h // 4
                            pv = PS[4 + h % 2]
                            pvv = pv.t[:, 0:392].rearrange("p (a b) -> p a b", a=4)
                            score = make_score(512, kcT.t[:, g, hv, 0:127], kcT.T, qraw[hp].t[:, qsl], qraw[hp].T,
                                               [(ident.t[0:127, 0:127], ident.T, mcmp.t[0:127, i, :], mcmp.T)], 127)

                            def pvf(pt, g=g, pv=pv, pvv=pvv):
                                for qs in range(4):
                                    mm(pv, pvv[:, qs, :], pt.t[0:127, qs * 128:(qs + 1) * 128], vcx.t[0:127, g, :], qs == 0, qs == 3,
                                       [pt.T, vcx.T], skip=True)

                            def fin(h=h, g=g, pv=pv, pvv=pvv):
                                finish_head(pv, pvv, i, h, 0, True, True)
                                if i >= 2:
                                    rb = rcp.t[:].rearrange("p (a o) -> p a o", o=1).to_broadcast([128, 4, 32])
                                    if h % 4 == 0:
                                        S.op("dve", lambda: V.tensor_tensor(imp.t[:, :, g, :], pvv[:, :, 66:98], rb, ALU.mult), [pv.T, rcp.T], [imp.T])
                                    else:
                                        S.op("dve", lambda: V.tensor_tensor(tmpi.t[:], pvv[:, :, 66:98], rb, ALU.mult), [pv.T, rcp.T], [tmpi.T])
                                        S.op("dve", lambda: V.tensor_tensor(imp.t[:, :, g, :], imp.t[:, :, g, :], tmpi.t[:], ALU.add), [imp.T, tmpi.T], [imp.T])

                            blocks.append((score, pvf, fin))
                        run_blocks(blocks, "cmp")
                        if i >= 2:
                            tsl = slice((i - 2) * 4, (i - 2) * 4 + 4)
                            for g in range(2):
                                S.op("dve", lambda: V.tensor_tensor(score_t.t[:], imp.t[:, :, g, :], stab.t[:, 0, tsl, :], ALU.add), [imp.T, stab.T], [score_t.T])
                                S.op("dve", lambda: V.tensor_tensor(score_t.t[:], score_t.t[:], stab.t[:, 1, tsl, :], ALU.mult), [score_t.T, stab.T], [score_t.T])
                                S.op("dve", lambda: V.tensor_tensor(score_t.t[:], score_t.t[:], stab.t[:, 2, tsl, :], ALU.add), [score_t.T, stab.T], [score_t.T])
                                for qs in range(4):
                                    S.op("dve", lambda: V.max(out=mx8.t[:, 0:8], in_=score_t.t[:, qs, :]), [score_t.T], [mx8.T])
                                    S.op("dve", lambda: V.match_replace(out=mrep.t[:], in_to_replace=mx8.t[:, 0:8], in_values=score_t.t[:, qs, :],
                                                                        imm_value=-1e9), [score_t.T, mx8.T], [mrep.T])
                                    S.op("dve", lambda: V.max(out=mx8.t[:, 8:16], in_=mrep.t[:]), [mrep.T], [mx8.T])
                                    S.op("dve", lambda: V.tensor_scalar(sel.t[:], score_t.t[:, qs, :], mx8.t[:, 15:16], None, ALU.is_ge), [score_t.T, mx8.T], [sel.T])
                                    S.op("dve", lambda: V.tensor_scalar(selb.t[:], sel.t[:], -NEGM, NEGM, ALU.mult, ALU.add), [sel.T], [selb.T])
                                    pb = next_pb()
                                    S.op("pe", lambda: PE.transpose(pb.t[0:32, 0:128], selb.t[:, :], ident.t[:]), [selb.T, ident.T], [pb.T])
                                    S.op("act", lambda: A.activation(out=selbT[g].t[0:32, qs * 128:(qs + 1) * 128], in_=pb.t[0:32, 0:128], func=AF.Copy),
                                         [pb.T], [selbT[g].T])
                        blocks = []
                        for br in (1, 2):
                            for h in range(8):
                                hp, hv, g = h // 2, h % 2, h // 4
                                pv = PS[4 + h % 2]
                                pvv = pv.t[:, 0:260].rearrange("p (a b) -> p a b", a=4)
                                plan = []
                                if br == 1:
                                    for kb in range(0, 4 * i + 4):
                                        c = kb - 4 * i
                                        lo = max(0, 128 * c)
                                        extra = []
                                        if i >= 2:
                                            extra.append((selE.t[:, kb, :], selE.T, selbT[g].t[:, lo:512], selbT[g].T))
                                        if c >= 0:
                                            extra.append((ident.t[:], ident.T, msk.t[:, c, lo:512], msk.T))
                                        plan.append((kb, lo, 512, extra))
                                    kk_ = ksl[g][hv]
                                    vidx = 0 + g
                                else:
                                    for kb in range(max(0, 4 * i - 4), 4 * i + 4):
                                        c = kb - 4 * i
                                        if c >= 0:
                                            plan.append((kb, 128 * c, 512, [(ident.t[:], ident.T, msk.t[:, c, 128 * c:512], msk.T)]))
                                        else:
                                            cp = c + 4
                                            plan.append((kb, 0, 128 * (cp + 1), [(ident.t[:], ident.T, msk.t[:, 4 + cp, 0:128 * (cp + 1)], msk.T)]))
                                    kk_ = kwn[g][hv]
                                    vidx = 2 + g
                                nmm = sum((hi - lo) // 128 for (_, lo, hi, _) in plan)
                                kcnt = [0]
                                for bi, (kb, lo, hi, extra) in enumerate(plan):
                                    score = make_score(hi - lo, kk_.t[:, kb * 128:(kb + 1) * 128], kk_.T,
                                                       qrot[hp].t[:, i * TQ + lo:i * TQ + hi], qrot[hp].T, extra)

                                    def pvf(pt, kb=kb, lo=lo, hi=hi, vidx=vidx, pv=pv, pvv=pvv, kcnt=kcnt, nmm=nmm):
                                        for qs in range(lo // 128, hi // 128):
                                            mm(pv, pvv[:, qs, :], pt.t[:, qs * 128 - lo:qs * 128 - lo + 128], vall.t[:, kb, vidx, :],
                                               kcnt[0] == 0, kcnt[0] == nmm - 1, [pt.T, vall.T], skip=True)
                                            kcnt[0] += 1

                                    def fin(h=h, br=br, pv=pv, pvv=pvv):
                                        finish_head(pv, pvv, i, h, br, False, False)

                                    blocks.append((score, pvf, fin if bi == len(plan) - 1 else None))
                        run_blocks(blocks, "slc")
                        ob = oabf[i % 2]
                        S.op("act", lambda: A.activation(out=ob.t[:], in_=oacc.t[:], func=AF.Copy), [oacc.T], [ob.T])
                        transpose_store(oTs[i % 2], ob, ob.T, 0, i, "oa")
                        if phase_end("P4q%d" % i):
                            return
                    S.barrier()

                if phase_end("P4"):
                    return
                av.close()
                with ExitStack() as st:
                    oall = sb(st, "oall", [128, 4, 4, SEQ], BF16)
                    oallT = [T("oall%d" % n) for n in range(4)]
                    for n in range(4):
                        for cc in range(4):
                            S.dma(oall.t[:, n, cc, :], scr_o[n, cc * 128:(cc + 1) * 128, :], reads=Tscr[n], writes=[oallT[n]])
                    mixT = sb(st, "mixT", [128, 8, SEQ], BF16)
                    stm = st.enter_context(ExitStack())
                    acc = [sb(stm, "macc%d" % i, [128, 512]) for i in range(NQT)]
                    sgm = [sb(stm, "msg%d" % i, [128, 512]) for i in range(2)]
                    prd = [sb(stm, "mprd%d" % i, [128, 512]) for i in range(2)]
                    k = 0
                    for dc in range(8):
                        for n in range(4):
                            wblk = load_wblock(D["w_merge"][l, dc, n], 1536)
                            for i in range(NQT):
                                sl = slice(i * TQ, (i + 1) * TQ)
                                pu = next_ps(0, 6)
                                pg = next_ps(0, 6)
                                for cc in range(4):
                                    mm(pu, pu.t[:, :], wblk.t[:, cc * 128:(cc + 1) * 128], oall.t[:, n, cc, sl], cc == 0, cc == 3,
                                       [wblk.T, oallT[n]])
                                for kc in range(8):
                                    mm(pg, pg.t[:, :], wblk.t[:, 512 + kc * 128:512 + (kc + 1) * 128], hT.t[:, kc, sl], kc == 0, kc == 7,
                                       [wblk.T, hTT[i]])
                                sg = sgm[k % 2]
                                pr = prd[k % 2]
                                k += 1
                                S.op("act", lambda: A.activation(out=sg.t[:], in_=pg.t[:], func=AF.Sigmoid), [pg.T], [sg.T])
                                if n == 0:
                                    S.op("dve", lambda: V.tensor_tensor(acc[i].t[:], sg.t[:], pu.t[:], ALU.mult), [sg.T, pu.T], [acc[i].T])
                                else:
                                    S.op("dve", lambda: V.tensor_tensor(pr.t[:], sg.t[:], pu.t[:], ALU.mult), [sg.T, pu.T], [pr.T])
                                    if n < 3:
                                        S.op("pool", lambda: P.tensor_tensor(acc[i].t[:], acc[i].t[:], pr.t[:], ALU.add), [acc[i].T, pr.T], [acc[i].T])
                                    else:
                                        S.op("pool", lambda: P.tensor_tensor(mixT.t[:, dc, sl], acc[i].t[:], pr.t[:], ALU.add),
                                             [acc[i].T, pr.T], [mixT.T])
                    S.barrier()
                    stm.close()
                    wo = sb(st, "wo", [128, 8, DM], BF16)
                    for pc in range(4):
                        load_cast(wo.t[:, pc * 2:(pc + 1) * 2, :].rearrange("p a b -> p (a b)"), wo.T, D["w_o"][l, pc], 2048)

                    def get_ps(tt):
                        pa = next_ps(0, 6)
                        pbk = next_ps(0, 6)
                        for hf, pp_ in enumerate((pa, pbk)):
                            for kc in range(8):
                                mm(pp_, pp_.t[:, :], mixT.t[:, kc, tt * 128:(tt + 1) * 128], wo.t[:, kc, hf * 512:(hf + 1) * 512],
                                   kc == 0, kc == 7, [mixT.T, wo.T])
                        return pa, pbk

                    out_epilogue(st, s, l, "norm_mix_post", get_ps, "mo")
                    S.barrier()
            if "x1" in tapd and s == 0 and l == 0:
                xt_ = None

        def ffn(s, l):
            with ExitStack() as fx:
                actT = sb(fx, "actT", [128, NFC, SEQ], BF16)
                with ExitStack() as st:
                    hfT = sb(st, "hfT", [128, 8, 2 + SEQ], BF16)
                    hfTT = [T("hfT%d" % i) for i in range(NQT)]
                    hall = T("hfTall")
                    S.op("pool", lambda: P.memset(hfT.t[:, :, 0:2], 0.0), [], hfTT)
                    with ExitStack() as st2:
                        norm_transpose(st2, s, l, lambda tt: y[s, tt * 128:(tt + 1) * 128, :], lambda tt: [Ty[s][tt]],
                                       "norm_ffn_pre", hfT, hfTT, 2)
                        S.barrier()
                    fcw = sb(st, "fcw", [128, 44, 3])
                    fcb = sb(st, "fcb", [128, 44])
                    S.dma(fcw.t[:].rearrange("p a b -> p (a b)"), D["ffn_cw"][l], writes=[fcw.T])
                    S.dma(fcb.t[:], D["ffn_cb"][l], writes=[fcb.T])
                    yg = [sb(st, "yg%d" % i, [128, 512]) for i in range(2)]
                    yv = [sb(st, "yv%d" % i, [128, 512]) for i in range(2)]
                    gl = [sb(st, "gl%d" % i, [128, 512]) for i in range(2)]
                    k = 0
                    for c in range(NFC):
                        wblk = load_wblock(D["w_up"][l, c], 2048)
                        for (st0, n) in FFN_TILES:
                            pg = next_ps(0, 6)
                            pv = next_ps(0, 6)
                            for half, pp_ in enumerate((pg, pv)):
                                for kc in range(8):
                                    mm(pp_, pp_.t[:, 0:n + 2], wblk.t[:, kc * 256 + half * 128:kc * 256 + half * 128 + 128],
                                       hfT.t[:, kc, st0:st0 + n + 2], kc == 0, kc == 7, hfTT + [wblk.T])
                            ygb, yvb, glb = yg[k % 2], yv[k % 2], gl[k % 2]
                            k += 1
                            for (pp_, yb, ch) in ((pg, ygb, c), (pv, yvb, NFC + c)):
                                S.op("act", lambda pp_=pp_, yb=yb, ch=ch: A.activation(
                                    out=yb.t[:, 0:n], in_=pp_.t[:, 2:n + 2], func=AF.Identity, scale=fcw.t[:, ch, 2:3], bias=fcb.t[:, ch:ch + 1]),
                                    [pp_.T, fcw.T, fcb.T], [yb.T])
                                S.op("dve", lambda pp_=pp_, yb=yb, ch=ch: V.scalar_tensor_tensor(
                                    yb.t[:, 0:n], pp_.t[:, 1:n + 1], fcw.t[:, ch, 1:2], yb.t[:, 0:n], ALU.mult, ALU.add),
                                    [pp_.T, fcw.T, yb.T], [yb.T])
                                S.op("dve", lambda pp_=pp_, yb=yb, ch=ch: V.scalar_tensor_tensor(
                                    yb.t[:, 0:n], pp_.t[:, 0:n], fcw.t[:, ch, 0:1], yb.t[:, 0:n], ALU.mult, ALU.add),
                                    [pp_.T, fcw.T, yb.T], [yb.T])
                            S.op("act", lambda: A.activation(out=glb.t[:, 0:n], in_=ygb.t[:, 0:n], func=AF.Gelu_apprx_tanh), [ygb.T], [glb.T])
                            S.op("pool", lambda: P.tensor_tensor(actT.t[:, c, st0:st0 + n], glb.t[:, 0:n], yvb.t[:, 0:n], ALU.mult),
                                 [glb.T, yvb.T], [actT.T])
                    S.barrier()
                with ExitStack() as st:
                    wd = sb(st, "wd", [128, NFC, DM], BF16)
                    for pc in range(11):
                        load_cast(wd.t[:, pc * 2:(pc + 1) * 2, :].rearrange("p a b -> p (a b)"), wd.T, D["w_down"][l, pc], 2048)

                    def get_ps(tt):
                        pa = next_ps(0, 6)
                        pbk = next_ps(0, 6)
                        for hf, pp_ in enumerate((pa, pbk)):
                            for kc in range(NFC):
                                mm(pp_, pp_.t[:, :], actT.t[:, kc, tt * 128:(tt + 1) * 128], wd.t[:, kc, hf * 512:(hf + 1) * 512],
                                   kc == 0, kc == NFC - 1, [actT.T, wd.T])
                        return pa, pbk

                    out_epilogue(st, s, l, "norm_ffn_post", get_ps, "fo")
                    S.barrier()

        for s in range(nseq):
            for l in range(nl):
                if not stopped[0]:
                    mixer(s, l)
                if not stopped[0] and not phase_end("P5"):
                    ffn(s, l)
                    if not (s == nseq - 1 and l == nl - 1):
                        S.rotate()
        S.finish(fin)
        print("program instructions:", S.ninst)
    return nc


_NP2BIR = {np.dtype(np.float32): F32, np.dtype(ml_dtypes.bfloat16): BF16}


def run(inputs, nseq_per_core=2, ncores=8, nl=2, taps=(), stop_after=None):
    w = prep_weights(inputs)
    wshapes = {k: (v.shape, _NP2BIR[v.dtype]) for k, v in w.items()}
    nc = build_program(nseq_per_core, nl, wshapes, taps, stop_after)
    x = np.ascontiguousarray(np.asarray(inputs["x"], np.float32))
    in_maps = []
    for c in range(ncores):
        m = dict(w)
        m["x"] = np.ascontiguousarray(x[c * nseq_per_core:(c + 1) * nseq_per_core])
        in_maps.append(m)
    res = run_bass_kernel_spmd(nc, in_maps, core_ids=list(range(ncores)))
    return res.results


def kernel(**inputs):
    results = run(inputs)
    return np.concatenate([r["y"] for r in results], axis=0).astype(np.float32)
```
